# Optimizing a Trainium2 kernel written in Bass

```python
import jax, jax.numpy as jnp
from jax import lax
import numpy as np

D_MODEL = 1024
BATCH = 4
SEQ = 8192
DEPTH = 2

N_MEM = 256
HEAD_DIM = 64
N_Q_HEADS = 8
N_KV_HEADS = 2
Q_PER_KV = N_Q_HEADS // N_KV_HEADS
ATTN_WIDTH = N_Q_HEADS * HEAD_DIM
KV_WIDTH = N_KV_HEADS * HEAD_DIM
WINDOW = 128
BLOCK = 128
ROPE_THETA = 10000.0
POOL_WINDOWS = (2, 4, 8, 16)
N_POOL_GROUPS = len(POOL_WINDOWS)
POOL_WIDTH = D_MODEL - ATTN_WIDTH
POOL_GROUP_DIM = POOL_WIDTH // N_POOL_GROUPS
MIX_WIDTH = ATTN_WIDTH + POOL_WIDTH
IN_WIDTH = ATTN_WIDTH + 2 * KV_WIDTH + POOL_WIDTH
X_HEADS = 4
X_HEAD_DIM = D_MODEL // X_HEADS
X_WIDTH = X_HEADS * X_HEAD_DIM
D_FF = 2816
FFN_RES = 0.5
EPS = 1e-6
MAX_POS_OFFSET = 1024
NEG = -1e30

kernel_name = "hymba_swa_sink_pool_macaron_xattn"


def rms_norm(x, g):
    xf = x.astype(jnp.float32)
    y = xf * lax.rsqrt(jnp.mean(xf * xf, axis=-1, keepdims=True) + EPS)
    return (y * g.astype(jnp.float32)).astype(x.dtype)


def swiglu(h, w_gate, w_up, w_down):
    return (jax.nn.silu(h @ w_gate) * (h @ w_up)) @ w_down


def rope_tables(positions):
    inv_freq = ROPE_THETA ** (-jnp.arange(0, HEAD_DIM, 2, dtype=jnp.float32) / HEAD_DIM)
    ang = positions.astype(jnp.float32)[..., None] * inv_freq
    return jnp.cos(ang)[:, :, None, :], jnp.sin(ang)[:, :, None, :]


def apply_rope(t, cos, sin):
    tf = t.astype(jnp.float32)
    t1, t2 = tf[..., : HEAD_DIM // 2], tf[..., HEAD_DIM // 2:]
    return jnp.concatenate([t1 * cos - t2 * sin, t2 * cos + t1 * sin], axis=-1).astype(t.dtype)


def sliding_window_sink_attention(q, k, v, sinks):
    B, S = q.shape[0], q.shape[1]
    nb = S // BLOCK
    qb = q.reshape(B, nb, BLOCK, N_KV_HEADS, Q_PER_KV, HEAD_DIM)

    def with_prev(t):
        tb = t.reshape(B, nb, BLOCK, N_KV_HEADS, HEAD_DIM)
        prev = jnp.pad(tb[:, :-1], ((0, 0), (1, 0), (0, 0), (0, 0), (0, 0)))
        return jnp.concatenate([prev, tb], axis=2)

    kk, vv = with_prev(k), with_prev(v)
    scale = HEAD_DIM ** -0.5
    s = jnp.einsum('bnqkgd,bnjkd->bkgnqj', qb, kk).astype(jnp.float32) * scale
    qi = jnp.arange(BLOCK)[:, None]
    kj = jnp.arange(2 * BLOCK)[None, :]
    diff = qi + BLOCK - kj
    blk = jnp.arange(nb)[:, None, None]
    key_abs = (blk - 1) * BLOCK + kj[None]
    mask = (diff >= 0)[None] & (diff < WINDOW)[None] & (key_abs >= 0)
    s = jnp.where(mask, s, NEG)
    sink = sinks.astype(jnp.float32).reshape(1, N_KV_HEADS, Q_PER_KV, 1, 1, 1)
    sink = jnp.broadcast_to(sink, s.shape[:-1] + (1,))
    p = jax.nn.softmax(jnp.concatenate([s, sink], axis=-1), axis=-1)[..., :-1]
    o = jnp.einsum('bkgnqj,bnjkd->bnqkgd', p.astype(v.dtype), vv)
    return o.reshape(B, S, ATTN_WIDTH)


def multiscale_pool(u, pool_w, pool_scale):
    B, S = u.shape[0], u.shape[1]
    ug = u.reshape(B, S, N_POOL_GROUPS, POOL_GROUP_DIM)
    uf = ug.astype(jnp.float32)
    c = jnp.pad(jnp.cumsum(uf, axis=1), ((0, 0), (1, 0), (0, 0), (0, 0)))
    t = jnp.arange(S)[:, None]
    w = jnp.array(POOL_WINDOWS, dtype=jnp.int32)[None, :]
    start = jnp.maximum(t + 1 - w, 0)
    g = jnp.arange(N_POOL_GROUPS)[None, :]
    window_sum = c[:, 1:] - c[:, start, g]
    count = jnp.minimum(t + 1, w).astype(jnp.float32)[None, :, :, None]
    pooled = (window_sum / count - uf).astype(u.dtype)
    mixed = jnp.einsum('bsgc,gcd->bsgd', pooled, pool_w)
    mixed = mixed * pool_scale.reshape(N_POOL_GROUPS, POOL_GROUP_DIM)
    return mixed.reshape(B, S, POOL_WIDTH)


def memory_cross_attention(h, mem_n, wq, wkv, wo):
    B, S = h.shape[0], h.shape[1]
    q = (h @ wq).reshape(B, S, X_HEADS, X_HEAD_DIM)
    kv = (mem_n @ wkv).reshape(B, mem_n.shape[1], 2, X_HEADS, X_HEAD_DIM)
    k, v = kv[:, :, 0], kv[:, :, 1]
    s = jnp.einsum('bshd,bmhd->bhsm', q, k).astype(jnp.float32) * (X_HEAD_DIM ** -0.5)
    p = jax.nn.softmax(s, axis=-1)
    o = jnp.einsum('bhsm,bmhd->bshd', p.astype(v.dtype), v).reshape(B, S, X_WIDTH)
    return o @ wo


def setup_inputs(seed: int = 0) -> dict:
    key = jax.random.key(seed)
    ks = jax.random.split(key, 32)
    f32 = jnp.float32
    L, D = DEPTH, D_MODEL

    def w(k, shape, fan_in):
        return jax.random.normal(k, shape, f32) * (fan_in ** -0.5)

    def gain(k, shape):
        return 1.0 + 0.05 * jax.random.normal(k, shape, f32)

    offset = jax.random.randint(ks[2], (BATCH, 1), 0, MAX_POS_OFFSET, dtype=jnp.int32)
    positions = offset + jnp.arange(SEQ, dtype=jnp.int32)[None, :]
    return {
        "x": jax.random.normal(ks[0], (BATCH, SEQ, D), f32),
        "mem": jax.random.normal(ks[1], (BATCH, N_MEM, D), f32),
        "positions": positions,
        "ffn1_norm": gain(ks[3], (L, D)),
        "ffn1_w_gate": w(ks[4], (L, D, D_FF), D),
        "ffn1_w_up": w(ks[5], (L, D, D_FF), D),
        "ffn1_w_down": w(ks[6], (L, D_FF, D), D_FF),
        "mix_norm": gain(ks[7], (L, D)),
        "w_in": w(ks[8], (L, D, IN_WIDTH), D),
        "attn_sinks": 0.5 * jax.random.normal(ks[9], (L, N_Q_HEADS), f32),
        "pool_w": w(ks[10], (L, N_POOL_GROUPS, POOL_GROUP_DIM, POOL_GROUP_DIM), POOL_GROUP_DIM),
        "pool_scale": gain(ks[11], (L, POOL_WIDTH)),
        "attn_out_norm": gain(ks[12], (L, ATTN_WIDTH)),
        "pool_out_norm": gain(ks[13], (L, POOL_WIDTH)),
        "w_out": w(ks[14], (L, MIX_WIDTH, D), MIX_WIDTH),
        "xattn_norm": gain(ks[15], (L, D)),
        "mem_norm": gain(ks[16], (L, D)),
        "xattn_wq": w(ks[17], (L, D, X_WIDTH), D),
        "xattn_wkv": w(ks[18], (L, D, 2 * X_WIDTH), D),
        "xattn_wo": w(ks[19], (L, X_WIDTH, D), X_WIDTH),
        "ffn2_norm": gain(ks[20], (L, D)),
        "ffn2_w_gate": w(ks[21], (L, D, D_FF), D),
        "ffn2_w_up": w(ks[22], (L, D, D_FF), D),
        "ffn2_w_down": w(ks[23], (L, D_FF, D), D_FF),
        "final_norm": gain(ks[24], (D,)),
    }


def reference(x, mem, positions, ffn1_norm, ffn1_w_gate, ffn1_w_up, ffn1_w_down,
              mix_norm, w_in, attn_sinks, pool_w, pool_scale, attn_out_norm, pool_out_norm,
              w_out, xattn_norm, mem_norm, xattn_wq, xattn_wkv, xattn_wo,
              ffn2_norm, ffn2_w_gate, ffn2_w_up, ffn2_w_down, final_norm):
    B, S = x.shape[0], x.shape[1]
    cos, sin = rope_tables(positions)
    for l in range(DEPTH):
        x = x + FFN_RES * swiglu(rms_norm(x, ffn1_norm[l]), ffn1_w_gate[l], ffn1_w_up[l], ffn1_w_down[l])

        h = rms_norm(x, mix_norm[l])
        proj = h @ w_in[l]
        q = proj[..., :ATTN_WIDTH].reshape(B, S, N_Q_HEADS, HEAD_DIM)
        k = proj[..., ATTN_WIDTH:ATTN_WIDTH + KV_WIDTH].reshape(B, S, N_KV_HEADS, HEAD_DIM)
        v = proj[..., ATTN_WIDTH + KV_WIDTH:ATTN_WIDTH + 2 * KV_WIDTH].reshape(B, S, N_KV_HEADS, HEAD_DIM)
        u = proj[..., ATTN_WIDTH + 2 * KV_WIDTH:]
        q = apply_rope(q, cos, sin)
        k = apply_rope(k, cos, sin)
        out_a = sliding_window_sink_attention(q, k, v, attn_sinks[l])
        out_b = multiscale_pool(u, pool_w[l], pool_scale[l])
        merged = jnp.concatenate([rms_norm(out_a, attn_out_norm[l]),
                                  rms_norm(out_b, pool_out_norm[l])], axis=-1)
        x = x + merged @ w_out[l]

        x = x + memory_cross_attention(rms_norm(x, xattn_norm[l]), rms_norm(mem, mem_norm[l]),
                                       xattn_wq[l], xattn_wkv[l], xattn_wo[l])

        x = x + FFN_RES * swiglu(rms_norm(x, ffn2_norm[l]), ffn2_w_gate[l], ffn2_w_up[l], ffn2_w_down[l])
    return rms_norm(x, final_norm)
```

```python
import math
import contextlib
import numpy as np
import concourse.bass as bass
import concourse.mybir as mybir
from concourse.bass_utils import run_bass_kernel_spmd

F32 = mybir.dt.float32
BF16 = mybir.dt.bfloat16
I32 = mybir.dt.int32
AF = mybir.ActivationFunctionType
ALU = mybir.AluOpType

D = 1024
DFF = 2816
NCH = 8
HCH = 22
L = 2
HALO = 256
TT = 512
NMEM = 256
EPS = 1e-6
NVL = 52
V_FIN = 104
V_INVF = 112
V_SGN = 113
V_NPISGN = 114
V_NPI = 115
V_EPS = 116
NV = 120
NEG = -30000.0
NSLOT = 5
NYB = 3
NPT = 12
PI = math.pi


class SemBox:
    def __init__(self, sem):
        self.sem = sem
        self.cnt = 0


class Buf:
    def __init__(self, name, aliases=()):
        self.name = name
        self.w = None
        self.r = {}
        self.aliases = list(aliases)
        self.sbox = None

    def add_reader(self, tok):
        k, v = tok
        if self.r.get(k, (None, 0))[1] < v:
            self.r[k] = tok


class Eng:
    def __init__(self, name, h, sem):
        self.name = name
        self.h = h
        self.sem = sem
        self.cnt = 0
        self.seen = {}


class Prog:
    def __init__(self, nc, es):
        self.nc = nc
        self.es = es
        self.nsem = 0
        self.E = {}
        for name, h in (("pe", nc.tensor), ("act", nc.scalar), ("dve", nc.vector), ("pool", nc.gpsimd), ("sp", nc.sync)):
            self.E[name] = Eng(name, h, self.new_sem("e_" + name))
        self.sems = {}
        for e in self.E.values():
            self.sems[id(e.sem)] = e.sem
        self.ninst = 0

    def new_sem(self, name):
        self.nsem += 1
        return self.es.enter_context(self.nc.semaphore(name))

    def new_box(self, name):
        s = self.new_sem(name)
        if not hasattr(self, "sems"):
            self.sems = {}
        self.sems[id(s)] = s
        return SemBox(s)

    def sb(self, name, shape, dt):
        return self.es.enter_context(self.nc.sbuf_tensor("sb_" + name, shape, dt))

    def _deps(self, reads, writes):
        deps = []
        for b in reads:
            if b.w is not None:
                deps.append(b.w)
        for b in writes:
            for bb in (b, *b.aliases):
                if bb.w is not None:
                    deps.append(bb.w)
                deps.extend(bb.r.values())
        return deps

    def _wait(self, e, deps):
        need = {}
        for k, v in deps:
            if k == id(e.sem) and e.name == "pe":
                continue
            if need.get(k, 0) < v:
                need[k] = v
        for k, v in need.items():
            if e.seen.get(k, 0) >= v:
                continue
            e.h.wait_ge(self.sems[k], v)
            e.seen[k] = v
            self.ninst += 1

    def op(self, en, fn, reads=(), writes=(), signal=True):
        e = self.E[en]
        self._wait(e, self._deps(reads, writes))
        inst = fn(e.h)
        self.ninst += 1
        if signal:
            inst.then_inc(e.sem, 1)
            e.cnt += 1
            tok = (id(e.sem), e.cnt)
        else:
            tok = (id(e.sem), e.cnt + 1)
        for b in writes:
            b.w = tok
            b.r = {}
        for b in reads:
            b.add_reader(tok)
        return tok

    def dma_multi(self, qn, pairs, reads=(), writes=(), box=None):
        e = self.E[qn]
        self._wait(e, self._deps(reads, writes))
        if box is None:
            b0 = writes[0]
            if b0.sbox is None:
                b0.sbox = self.new_box("d_" + b0.name)
            box = b0.sbox
        for out_ap, in_ap in pairs:
            e.h.dma_start(out=out_ap, in_=in_ap).then_inc(box.sem, 16)
            self.ninst += 1
            box.cnt += 16
        tok = (id(box.sem), box.cnt)
        for b in writes:
            b.w = tok
            b.r = {}
        for b in reads:
            b.add_reader(tok)
        return tok

    def dma(self, qn, out_ap, in_ap, reads=(), writes=(), box=None):
        e = self.E[qn]
        self._wait(e, self._deps(reads, writes))
        if box is None:
            b0 = writes[0]
            if b0.sbox is None:
                b0.sbox = self.new_box("d_" + b0.name)
            box = b0.sbox
        inst = e.h.dma_start(out=out_ap, in_=in_ap)
        inst.then_inc(box.sem, 16)
        self.ninst += 1
        box.cnt += 16
        tok = (id(box.sem), box.cnt)
        for b in writes:
            b.w = tok
            b.r = {}
        for b in reads:
            b.add_reader(tok)
        return tok


def build_program(n_own):
    NT = n_own + HALO
    tiles = []
    c = 0
    while c < NT:
        T = min(TT, NT - c)
        tiles.append((c, T))
        c += T

    nc = bass.Bass("TRN2", target_bir_lowering=False)

    def din(name, shape, dt=F32):
        return nc.dram_tensor(name, list(shape), dt, kind="ExternalInput").ap()

    xT_d = din("xT", [128, NCH, NT])
    memT_d = din("memT", [128, NCH, NMEM])
    posr_d = din("posr", [128, NT], I32)
    vecs_d = din("vecs", [128, NV])
    sinkr_d = din("sinkr", [L, 128, 512])
    maskc_d = din("maskc", [128, 512])
    maskp_d = din("maskp", [128, 512])
    maskf_d = din("maskf", [128, 512])
    ident_d = din("ident", [128, 128])
    invcf_d = din("invcf", [128, 80])
    fW = {}
    for f in (1, 2):
        fW[(f, "g")] = din(f"f{f}g", [L, D, DFF])
        fW[(f, "u")] = din(f"f{f}u", [L, D, DFF])
        fW[(f, "d")] = din(f"f{f}d", [L, DFF, D])
    winx_d = din("winx", [L, D, 1920])
    poolw_d = din("poolw", [L, 4, 128, 128])
    wout_d = din("wout", [L, D, D])
    wq_d = din("wq", [L, D, D])
    wkv_d = din("wkv", [L, D, 2 * D])
    wo_d = din("wo", [L, D, D])
    yT_d = nc.dram_tensor("yT", [128, NCH, n_own], F32, kind="ExternalOutput").ap()

    es = contextlib.ExitStack()
    with es:
        P = Prog(nc, es)

        class _PV:
            def __init__(self, fn):
                self.fn = fn

        class _PieceAP:
            def __getitem__(self, key):
                return _PV(lambda slot, key=key: slot[key])

        class Piece:
            def __init__(self, name, ncols):
                self.ap = _PieceAP()
                self.dram = None
                self.gbox = None
                self.name = name
                self.n = ncols
                self.buf = Buf(name)
                self.conv = []

        def kcp(src2d, kc):
            return src2d.rearrange("(kc p) i -> p kc i", p=128)

        def pview(piece, lo, kc, i):
            return _PV(lambda slot, lo=lo, kc=kc, i=i: slot[:, lo:lo + kc * i].rearrange("p (kc i) -> p kc i", kc=kc))

        groups = []
        PCS = {}
        for l in range(L):
            def ffn_pieces(f):
                gu = []
                for j in range(11):
                    pc = Piece(f"s_f{f}gu_{l}_{j}", 4096)
                    pc.conv.append((pview(pc, 0, 8, 256), kcp(fW[(f, "g")][l][:, j * 256:(j + 1) * 256], 8)))
                    pc.conv.append((pview(pc, 2048, 8, 256), kcp(fW[(f, "u")][l][:, j * 256:(j + 1) * 256], 8)))
                    gu.append(pc)
                dn = []
                for m in range(8):
                    pc = Piece(f"s_f{f}d_{l}_{m}", 22 * 128)
                    pc.conv.append((pview(pc, 0, 22, 128), kcp(fW[(f, "d")][l][:, m * 128:(m + 1) * 128], 22)))
                    dn.append(pc)
                return gu, dn

            def col_pieces(name, src, chunk_groups):
                out = []
                for gi, chunks in enumerate(chunk_groups):
                    pc = Piece(f"{name}_{l}_{gi}", len(chunks) * 1024)
                    for ci, ch in enumerate(chunks):
                        pc.conv.append((pview(pc, ci * 1024, 8, 128), kcp(src[l][:, ch * 128:(ch + 1) * 128], 8)))
                    out.append(pc)
                return out

            gu1, dn1 = ffn_pieces(1)
            groups.append((f"f1a_{l}", gu1[:6]))
            groups.append((f"f1b_{l}", gu1[6:] + dn1))
            win = col_pieces("s_win", winx_d, [[0, 1, 2, 3], [4, 5, 6, 7], [8, 9, 10, 11], [12, 13, 14]])
            pw = Piece(f"s_pw_{l}", 512)
            pw.conv.append((_PV(lambda slot: slot[:, 0:512].rearrange("c (g d) -> c g d", g=4)), poolw_d[l].rearrange("g c d -> c g d")))
            wout = col_pieces("s_wout", wout_d, [[0, 1, 2, 3], [4, 5, 6, 7]])
            groups.append((f"mix_{l}", win + [pw] + wout))
            wkvk = col_pieces("s_wkvk", wkv_d, [[0, 1, 2, 3], [4, 5, 6, 7]])
            wkvv = []
            for hf in range(2):
                pc = Piece(f"s_wkvv_{l}_{hf}", 4096)
                pc.conv.append((pview(pc, 0, 8, 512), kcp(wkv_d[l][:, D + hf * 512:D + (hf + 1) * 512], 8)))
                wkvv.append(pc)
            wq = col_pieces("s_wq", wq_d, [[0, 1, 2, 3], [4, 5, 6, 7]])
            wo = col_pieces("s_wo", wo_d, [[0, 1, 2, 3], [4, 5, 6, 7]])
            groups.append((f"xat_{l}", wkvk + wkvv + wq + wo))
            gu2, dn2 = ffn_pieces(2)
            groups.append((f"f2a_{l}", gu2[:6]))
            groups.append((f"f2b_{l}", gu2[6:] + dn2))
            PCS[l] = dict(gu1=gu1, dn1=dn1, win=win, pw=pw, wout=wout, wkvk=wkvk, wkvv=wkvv, wq=wq, wo=wo, gu2=gu2, dn2=dn2)

        conv_state = {"next": 0}
        ntiles_total = len(tiles)

        def issue_conv(upto):
            return

        STATE = {"ti": 0}
        for gname, pcs_ in groups:
            if gname.startswith("xat"):
                pcs_ = pcs_[4:]
            for pc in pcs_:
                pc.dram = nc.dram_tensor(pc.name, [128, pc.n], BF16, kind="Internal").ap()

        xTs = [P.sb(f"xT{i}", [128, NCH, TT], F32) for i in range(2)]
        xbs = [[Buf(f"x{i}_{c}") for c in range(NCH)] for i in range(2)]
        xT = xTs[0]
        xb = xbs[0]
        ybuf = [P.sb(f"ybuf{i}", [128, TT], F32) for i in range(NYB)]
        ybufb = [Buf(f"ybuf{i}") for i in range(NYB)]
        hT = P.sb("hT", [128, NCH, TT], BF16)
        hb = [Buf(f"h{c}") for c in range(NCH)]
        sq = P.sb("sq", [128, NCH, TT], BF16)
        sqb = [Buf(f"sq{c}") for c in range(NCH)]
        act2 = P.sb("act", [128, HCH * TT], BF16)
        actb = [Buf(f"act{k}") for k in range(HCH)]
        act32 = act2[:, :].bitcast(F32)

        def actc(k, T=TT):
            return act2[:, k * TT:k * TT + T]

        def act_f32(k0):
            return act32[:, k0 * 256:k0 * 256 + 512]

        sg = [P.sb(f"sg{i}", [128, TT], F32) for i in range(2)]
        sgb = [Buf(f"sg{i}") for i in range(2)]
        tmp = [P.sb(f"tmp{i}", [128, TT], F32) for i in range(6)]
        tmpb = [Buf(f"tmp{i}") for i in range(6)]
        rst = [P.sb(f"rst{i}", [128, TT], F32) for i in range(3)]
        rstb = [Buf(f"rst{i}") for i in range(3)]
        kT = P.sb("kT", [128, 640], BF16)
        kcarb, kcurb = Buf("kcar"), Buf("kcur")
        vtm = P.sb("vtm", [128, 5, 128], BF16)
        vcarb, vcurb = Buf("vcar"), Buf("vcur")
        ubuf = P.sb("ubuf", [128, 4, 528], F32)
        ucarb = [Buf(f"ucar{g}") for g in range(4)]
        ucurb = [Buf(f"ucur{g}") for g in range(4)]
        kcarL = [P.sb(f"kcarL{l}", [128, 128], BF16) for l in range(L)]
        vcarL = [P.sb(f"vcarL{l}", [128, 128], BF16) for l in range(L)]
        ucarL = [P.sb(f"ucarL{l}", [128, 4, 16], F32) for l in range(L)]
        kcarLb = [Buf(f"kcarL{l}") for l in range(L)]
        vcarLb = [Buf(f"vcarL{l}") for l in range(L)]
        ucarLb = [Buf(f"ucarL{l}") for l in range(L)]
        ws = [P.sb(f"ws{i}", [128, 528], F32) for i in range(2)]
        wsb = [Buf(f"ws{i}") for i in range(2)]
        pooled = P.sb("pooled", [128, 4, TT], BF16)
        pooledb = [Buf(f"pooled{g}") for g in range(4)]
        mrg2 = P.sb("mrg", [128, NCH * TT], BF16)
        mrgb = [Buf(f"mrg{c}") for c in range(NCH)]
        memx = mrg2[:, :].bitcast(F32)

        def mrgc(c, T=TT):
            return mrg2[:, c * TT:c * TT + T]
        pT = [P.sb(f"pT{i}", [128, TT], BF16) for i in range(NPT)]
        pTb = [Buf(f"pT{i}") for i in range(NPT)]
        cos2 = P.sb("cos2", [128, TT], F32)
        sinS = P.sb("sinS", [128, TT], F32)
        cosb, sinb = Buf("cos2"), Buf("sinS")
        maskc = P.sb("maskc", [128, 512], BF16)
        maskp = P.sb("maskp", [128, 512], BF16)
        maskf = P.sb("maskf", [128, 512], BF16)
        ident = P.sb("ident", [128, 128], BF16)
        ones = P.sb("ones", [128, 128], BF16)
        constb = Buf("consts")
        constb2 = Buf("consts2")
        esink = [P.sb(f"esink{l}", [128, 512], F32) for l in range(L)]
        esinkb = [Buf(f"esink{l}") for l in range(L)]
        KTx = [P.sb(f"KTx{l}", [128, NCH, NMEM], BF16) for l in range(L)]
        Vx = [P.sb(f"Vx{l}", [128, 2, D], BF16) for l in range(L)]
        KTxb = [Buf(f"KTx{l}") for l in range(L)]
        Vxb = [Buf(f"Vx{l}") for l in range(L)]
        vecs = P.sb("vecs", [128, NV], F32)
        dummy = P.sb("dummy", [128, 2], F32)
        dummyb = Buf("dummy")
        invcf = P.sb("invcf", [128, 80], F32)
        ring = [P.sb(f"ring{i}", [128, 4096], BF16) for i in range(NSLOT)]
        ringb = [Buf(f"ring{i}") for i in range(NSLOT)]
        ring_state = {"i": 0}
        wbbox = [P.new_box(f"wb{i}") for i in range(NSLOT)]
        ring_sw = [P.new_box(f"ringsw{i}") for i in range(NSLOT)]

        qrb = [Buf(f"qr{c}", aliases=[actb[c]]) for c in range(4)]
        aob = [Buf(f"ao{j}", aliases=[actb[4 + 2 * j], actb[5 + 2 * j]]) for j in range(4)]
        obb = [Buf(f"ob{g}", aliases=[actb[12 + 2 * g], actb[13 + 2 * g]]) for g in range(4)]
        qxb = [Buf(f"qx{c}", aliases=[actb[c]]) for c in range(NCH)]
        oxb = [Buf(f"ox{c}", aliases=[actb[8 + c]]) for c in range(NCH)]
        youtb = [Buf(f"yout{c}", aliases=[actb[2 * c], actb[2 * c + 1]]) for c in range(NCH)]
        memxb = [Buf(f"memx{c}", aliases=[mrgb[c]]) for c in range(NCH)]
        for k in range(HCH):
            al = []
            for lst in (qrb, aob, obb, qxb, oxb, youtb):
                for b in lst:
                    if actb[k] in b.aliases:
                        al.append(b)
            actb[k].aliases = al
        for c in range(NCH):
            mrgb[c].aliases = [memxb[c]]
        overlays = qrb + aob + obb + qxb + oxb + youtb
        for b in overlays:
            extra = []
            for o in overlays:
                if o is not b and any(a in b.aliases for a in o.aliases if a in actb):
                    extra.append(o)
            b.aliases = b.aliases + extra

        def qr_ap(c):
            return actc(c)

        def ao_ap(j):
            return act_f32(4 + 2 * j)

        def ob_ap(g):
            return act_f32(12 + 2 * g)

        def qx_ap(c):
            return actc(c)

        def ox_ap(c):
            return actc(8 + c)

        def yout_ap(c):
            return act_f32(2 * c)

        psum = [es.enter_context(nc.psum_tensor(f"ps{i}", [128, 512], F32)) for i in range(8)]
        psb = [Buf(f"ps{i}") for i in range(8)]
        ps_state = {"i": 0}

        def ps_next():
            i = ps_state["i"]
            ps_state["i"] = (i + 1) % 8
            return psum[i], psb[i]

        rot = {"sg": 0, "rst": 0, "pT": 0, "tmp": 0}

        def nxt(kind, n):
            i = rot[kind]
            rot[kind] = (i + 1) % n
            return i

        def vcol(c):
            return vecs[:, c:c + 1]

        def mm_group(out_ap, out_buf, terms, extra_reads=(), first=True, final=True):
            n = len(terms)
            for i, (lt, rh, rb) in enumerate(terms):
                last = (i == n - 1) and final
                st = (i == 0) and first
                if first and final:
                    P.op("pe", lambda h, lt=lt, rh=rh, st=st, last=last: h.matmul(out_ap, lt, rh, start=st, stop=last),
                         reads=list(rb) + list(extra_reads), writes=(out_buf,), signal=last)
                else:
                    P.op("pe", lambda h, lt=lt, rh=rh, st=st, last=last: h.matmul(out_ap, lt, rh, start=st, stop=last, skip_group_check=True),
                         reads=list(rb) + list(extra_reads), writes=(out_buf,), signal=last)

        def load_piece(pc):
            i = ring_state["i"]
            ring_state["i"] = (i + 1) % NSLOT
            if STATE["ti"] == 0 or pc.dram is None:
                P.dma_multi("pool", [(dv.fn(ring[i]), src) for dv, src in pc.conv], reads=(), writes=(ringb[i],), box=ring_sw[i])
                if pc.dram is not None and ntiles_total > 1:
                    P.dma("sp", pc.dram, ring[i][:, 0:pc.n], reads=(ringb[i],), writes=(), box=wbbox[i])
            else:
                P.dma("sp", ring[i][:, 0:pc.n], pc.dram, reads=(pc.buf,), writes=(ringb[i],))
            return ring[i], ringb[i]

        def rstd_sq(srcs, T):
            for c, (ap, b) in enumerate(srcs):
                P.op("act", lambda h, ap=ap, c=c: h.activation(out=sq[:, c, :T], in_=ap, func=AF.Square),
                     reads=(b,), writes=(sqb[c],))

        def rstd_from(srcs, Dn, T, skip_sq=False, chunks=None):
            n = len(srcs) if chunks is None else len(chunks)
            if not skip_sq:
                rstd_sq(srcs, T)
            pn, pnb = ps_next()
            cids = list(range(n)) if chunks is None else list(chunks)
            mm_group(pn[:, :T], pnb, [(ones[:, :], sq[:, c, :T], (sqb[c], constb)) for c in cids])
            i = nxt("rst", 3)
            j = nxt("rst", 3)
            P.op("act", lambda h: h.activation(out=rst[i][:, :T], in_=pn[:, :T], func=AF.Ln, bias=vcol(V_EPS), scale=1.0 / Dn),
                 reads=(pnb, constb), writes=(rstb[i],))
            P.op("act", lambda h: h.activation(out=rst[j][:, :T], in_=rst[i][:, :T], func=AF.Exp, scale=-0.5),
                 reads=(rstb[i],), writes=(rstb[j],))
            return rst[j][:, :T], rstb[j]

        def norm_x_to_h(gbase, T, xT_=None, xb_=None):
            xT_ = xT if xT_ is None else xT_
            xb_ = xb if xb_ is None else xb_
            r_ap, r_b = rstd_from([(xT_[:, c, :T], xb_[c]) for c in range(NCH)], D, T)
            for c in range(NCH):
                P.op("dve", lambda h, c=c: h.scalar_tensor_tensor(out=hT[:, c, :T], in0=xT_[:, c, :T], scalar=vcol(gbase + c),
                                                                   in1=r_ap, op0=ALU.mult, op1=ALU.mult),
                     reads=(xb_[c], r_b, constb), writes=(hb[c],))

        def ffn(l, f, T, pre_normed=False, hook_mid=None, hook_early=None, drip=None, hooks=None):
            pcs = PCS[l]
            gu, dn = (pcs["gu1"], pcs["dn1"]) if f == 1 else (pcs["gu2"], pcs["dn2"])
            if not pre_normed:
                norm_x_to_h(l * NVL + (0 if f == 1 else 32), T)
            for j in range(11):
                if j == 3 and hook_early is not None:
                    hook_early()
                if hooks is not None and j in hooks:
                    hooks[j]()
                slot, slb = load_piece(gu[j])
                for hh in range(2):
                    hc = 2 * j + hh
                    pg, pgb = ps_next()
                    pu, pub = ps_next()
                    mm_group(pg[:, :T], pgb, [(slot[:, kc * 256 + hh * 128: kc * 256 + hh * 128 + 128], hT[:, kc, :T], (slb, hb[kc])) for kc in range(8)])
                    mm_group(pu[:, :T], pub, [(slot[:, 2048 + kc * 256 + hh * 128: 2048 + kc * 256 + hh * 128 + 128], hT[:, kc, :T], (slb, hb[kc])) for kc in range(8)])
                    si = nxt("sg", 2)
                    P.op("act", lambda h, si=si, pg=pg: h.activation(out=sg[si][:, :T], in_=pg[:, :T], func=AF.Silu),
                         reads=(pgb,), writes=(sgb[si],))
                    P.op("dve", lambda h, si=si, pu=pu, hc=hc: h.tensor_tensor(out=actc(hc, T), in0=pu[:, :T], in1=sg[si][:, :T], op=ALU.mult),
                         reads=(pub, sgb[si]), writes=(actb[hc],))
                    if drip is not None:
                        next(drip, None)
            if drip is not None:
                for _ in drip:
                    pass
            if hook_mid is not None:
                hook_mid()
            P.op("act", lambda h: h.activation(out=dummy[:, 0:1], in_=vecs[:, V_EPS:V_EPS + 1], func=AF.Ln), reads=(constb,), writes=(dummyb,))
            for m in range(8):
                slot, slb = load_piece(dn[m])
                pd, pdb = ps_next()
                mm_group(pd[:, :T], pdb, [(slot[:, kc * 128:(kc + 1) * 128], actc(kc, T), (slb, actb[kc])) for kc in range(HCH)])
                P.op("dve", lambda h, pd=pd, m=m: h.scalar_tensor_tensor(out=xT[:, m, :T], in0=pd[:, :T], scalar=0.5, in1=xT[:, m, :T],
                                                                          op0=ALU.mult, op1=ALU.add),
                     reads=(pdb, xb[m]), writes=(xb[m],))

        def rope_tables_gen(c0, T):
            posi = tmp[5][:, :].bitcast(I32)
            P.dma("act", posi[:, :T], posr_d[:, c0:c0 + T], reads=(), writes=(tmpb[5],))
            yield
            posf, posfb = tmp[4], tmpb[4]
            P.op("dve", lambda h: h.tensor_copy(out=posf[:, :T], in_=posi[:, :T]), reads=(tmpb[5],), writes=(posfb,))
            yield
            for (tbl, tb, shift, is_sin) in ((sinS, sinb, PI, True), (cos2, cosb, 1.5 * PI, False)):
                a, ab = tmp[0], tmpb[0]
                n_, nb_ = tmp[1], tmpb[1]
                ni = tmp[2][:, :].bitcast(I32)
                nib = tmpb[2]
                P.op("dve", lambda h, shift=shift: h.tensor_scalar(out=a[:, :T], in0=posf[:, :T], scalar1=vcol(V_INVF), scalar2=shift,
                                                                    op0=ALU.mult, op1=ALU.add), reads=(posfb, constb), writes=(ab,))
                yield
                P.op("dve", lambda h: h.tensor_scalar(out=n_[:, :T], in0=a[:, :T], scalar1=1.0 / (2 * PI), scalar2=None, op0=ALU.mult),
                     reads=(ab,), writes=(nb_,))
                yield
                P.op("dve", lambda h: h.tensor_copy(out=ni[:, :T], in_=n_[:, :T]), reads=(nb_,), writes=(nib,))
                yield
                P.op("dve", lambda h: h.tensor_copy(out=n_[:, :T], in_=ni[:, :T]), reads=(nib,), writes=(nb_,))
                yield
                P.op("dve", lambda h: h.scalar_tensor_tensor(out=a[:, :T], in0=n_[:, :T], scalar=-2 * PI, in1=a[:, :T], op0=ALU.mult, op1=ALU.add),
                     reads=(nb_, ab), writes=(ab,))
                yield
                P.op("dve", lambda h: h.tensor_scalar(out=n_[:, :T], in0=a[:, :T], scalar1=0.0, scalar2=2 * PI, op0=ALU.is_lt, op1=ALU.mult),
                     reads=(ab,), writes=(nb_,))
                yield
                P.op("dve", lambda h: h.tensor_tensor(out=a[:, :T], in0=a[:, :T], in1=n_[:, :T], op=ALU.add), reads=(ab, nb_), writes=(ab,))
                yield
                P.op("dve", lambda h: h.tensor_scalar(out=n_[:, :T], in0=a[:, :T], scalar1=2 * PI, scalar2=-2 * PI, op0=ALU.is_ge, op1=ALU.mult),
                     reads=(ab,), writes=(nb_,))
                yield
                P.op("dve", lambda h: h.tensor_tensor(out=a[:, :T], in0=a[:, :T], in1=n_[:, :T], op=ALU.add), reads=(ab, nb_), writes=(ab,))
                yield
                if is_sin:
                    P.op("act", lambda h, tbl=tbl: h.activation(out=tbl[:, :T], in_=a[:, :T], func=AF.Sin, bias=vcol(V_NPISGN), scale=vcol(V_SGN)),
                         reads=(ab, constb), writes=(tb,))
                    yield
                else:
                    P.op("act", lambda h, tbl=tbl: h.activation(out=tbl[:, :T], in_=a[:, :T], func=AF.Sin, bias=vcol(V_NPI), scale=1.0),
                         reads=(ab, constb), writes=(tb,))
                    yield

        def rope_tables(c0, T):
            for _ in rope_tables_gen(c0, T):
                pass

        def mixer(l, ti, T):
            pcs = PCS[l]
            vb = l * NVL
            nb = T // 128
            P.op("dve", lambda h: h.tensor_copy(out=kT[:, 0:128], in_=kcarL[l][:, :]), reads=(kcarLb[l],), writes=(kcarb,))
            P.op("dve", lambda h: h.tensor_copy(out=vtm[:, 0, :], in_=vcarL[l][:, :]), reads=(vcarLb[l],), writes=(vcarb,))
            for g in range(4):
                P.op("dve", lambda h, g=g: h.tensor_copy(out=ubuf[:, g, 0:16], in_=ucarL[l][:, g, :]), reads=(ucarLb[l],), writes=(ucarb[g],))
            norm_x_to_h(vb + 8, T)
            slots = {}

            def proj(chunk_idx, slot, slb):
                pp, ppb = ps_next()
                mm_group(pp[:, :T], ppb, [(slot[:, chunk_idx * 1024 + kc * 128: chunk_idx * 1024 + (kc + 1) * 128], hT[:, kc, :T], (slb, hb[kc])) for kc in range(8)])
                return pp, ppb

            def rope(pq, pqb, pr, prb, out_ap, out_buf):
                i1 = 0 + nxt("tmp", 2)
                i2 = 2 + i1
                P.op("dve", lambda h: h.tensor_tensor(out=tmp[i1][:, :T], in0=pr[:, :T], in1=sinS[:, :T], op=ALU.mult),
                     reads=(prb, sinb), writes=(tmpb[i1],))
                P.op("dve", lambda h: h.tensor_tensor(out=tmp[i2][:, :T], in0=pq[:, :T], in1=cos2[:, :T], op=ALU.mult),
                     reads=(pqb, cosb), writes=(tmpb[i2],))
                P.op("dve", lambda h: h.tensor_tensor(out=out_ap, in0=tmp[i1][:, :T], in1=tmp[i2][:, :T], op=ALU.add),
                     reads=(tmpb[i1], tmpb[i2]), writes=(out_buf,))

            slot, slb = load_piece(pcs["win"][2])
            pq, pqb = proj(0, slot, slb)
            pr, prb = proj(1, slot, slb)
            rope(pq, pqb, pr, prb, kT[:, 128:128 + T], kcurb)
            ups = []
            for g in range(2):
                pp, ppb = proj(2 + g, slot, slb)
                P.op("dve", lambda h, pp=pp, g=g: h.tensor_copy(out=ubuf[:, g, 16:16 + T], in_=pp[:, :T]),
                     reads=(ppb,), writes=(ucurb[g],))
            slot, slb = load_piece(pcs["win"][3])
            for g in range(2, 4):
                pp, ppb = proj(g - 2, slot, slb)
                P.op("dve", lambda h, pp=pp, g=g: h.tensor_copy(out=ubuf[:, g, 16:16 + T], in_=pp[:, :T]),
                     reads=(ppb,), writes=(ucurb[g],))
            vslot, vslb = slot, slb
            if ti == 0:
                for g in range(4):
                    P.op("dve", lambda h, g=g: h.tensor_tensor(out=ubuf[:, g, 16 + 240:16 + 256], in0=ubuf[:, g, 16 + 240:16 + 256],
                                                               in1=invcf[:, 64:80], op=ALU.mult),
                         reads=(ucurb[g], constb), writes=(ucurb[g],))
            for pi_ in range(2):
                slot, slb = load_piece(pcs["win"][pi_])
                for cc in range(2):
                    c = pi_ * 2 + cc
                    pq, pqb = proj(cc * 2, slot, slb)
                    pr, prb = proj(cc * 2 + 1, slot, slb)
                    rope(pq, pqb, pr, prb, qr_ap(c)[:, :T], qrb[c])
            pv, pvb = ps_next()
            for b in range(nb):
                mm_group(pv[:, b * 128:(b + 1) * 128], pvb,
                         [(hT[:, kc, b * 128:(b + 1) * 128], vslot[:, 2048 + kc * 128: 2048 + (kc + 1) * 128], (vslb, hb[kc])) for kc in range(8)])
            P.op("dve", lambda h: h.tensor_copy(out=vtm[:, 1:1 + nb, :].rearrange("p b d -> p (b d)"), in_=pv[:, :nb * 128]),
                 reads=(pvb,), writes=(vcurb,))
            pwslot, pwslb = load_piece(pcs["pw"])
            for g in range(4):
                nlev = g + 1
                w = 2 ** nlev
                src, srcb = ubuf[:, g, :], (ucarb[g], ucurb[g])
                for lev in range(nlev):
                    step = 2 ** lev
                    lo = 2 ** (lev + 1) - 1
                    dst, dstb = ws[lev % 2], wsb[lev % 2]
                    P.op("dve", lambda h, src=src, dst=dst, lo=lo, step=step: h.tensor_tensor(
                        out=dst[:, lo:16 + T], in0=src[:, lo:16 + T], in1=src[:, lo - step:16 + T - step], op=ALU.add),
                        reads=srcb, writes=(dstb,))
                    src, srcb = dst, (dstb,)
                P.op("dve", lambda h, src=src, g=g, w=w: h.scalar_tensor_tensor(out=pooled[:, g, :T], in0=src[:, 16:16 + T], scalar=1.0 / w,
                                                                                in1=ubuf[:, g, 16:16 + T], op0=ALU.mult, op1=ALU.subtract),
                     reads=tuple(srcb) + (ucurb[g],), writes=(pooledb[g],))
                if ti == 0:
                    i1 = nxt("tmp", 2)
                    P.op("dve", lambda h, src=src, g=g: h.tensor_tensor(out=tmp[i1][:, 0:16], in0=src[:, 16 + 256:16 + 272], in1=invcf[:, g * 16:(g + 1) * 16], op=ALU.mult),
                         reads=tuple(srcb) + (constb,), writes=(tmpb[i1],))
                    P.op("dve", lambda h, g=g: h.tensor_tensor(out=pooled[:, g, 256:272], in0=tmp[i1][:, 0:16], in1=ubuf[:, g, 16 + 256:16 + 272], op=ALU.subtract),
                         reads=(tmpb[i1], ucurb[g]), writes=(pooledb[g],))
            def pool_mm():
                for g in range(4):
                    pm, pmb = ps_next()
                    mm_group(pm[:, :T], pmb, [(pwslot[:, g * 128:(g + 1) * 128], pooled[:, g, :T], (pwslb, pooledb[g]))])
                    P.op("dve", lambda h, pm=pm, g=g: h.tensor_scalar(out=ob_ap(g)[:, :T], in0=pm[:, :T], scalar1=vcol(vb + 48 + g), scalar2=None, op0=ALU.mult),
                         reads=(pmb, constb), writes=(obb[g],))

            def bnorm_sq():
                rstd_sq([(ob_ap(g)[:, :T], obb[g]) for g in range(4)], T)

            def bnorm():
                rB, rBb = rstd_from([(ob_ap(g)[:, :T], obb[g]) for g in range(4)], 512, T, skip_sq=True)
                for g in range(4):
                    P.op("dve", lambda h, g=g: h.scalar_tensor_tensor(out=mrgc(4 + g, T), in0=ob_ap(g)[:, :T], scalar=vcol(vb + 44 + g), in1=rB,
                                                                       op0=ALU.mult, op1=ALU.mult),
                         reads=(obb[g], rBb, constb), writes=(mrgb[4 + g],))
            def scores(b):
                gb = ti * (TT // 128) + b
                mprev = maskf if gb == 2 else maskp
                pts = {}
                for g in range(2):
                    rhs_q = act2[g * 64:(g + 1) * 64, 0:4 * TT].rearrange("p (c t) -> p c t", t=TT)[:, :, b * 128:(b + 1) * 128]
                    qreads = tuple(qrb)
                    for which in range(2):
                        pS, pSb = ps_next()
                        kcols = slice(b * 128 + which * 128, b * 128 + which * 128 + 128)
                        mk = mprev if which == 0 else maskc
                        kreads = (kcarb, kcurb)
                        mm_group(pS[:, :], pSb, [(ident[:, :], mk[:, :], (constb2,)),
                                                 (kT[g * 64:(g + 1) * 64, kcols], rhs_q, kreads + qreads)])
                        pi2 = nxt("pT", NPT)
                        P.op("act", lambda h, pS=pS, pi2=pi2: h.activation(out=pT[pi2][:, :], in_=pS[:, :], func=AF.Exp, scale=0.125),
                             reads=(pSb,), writes=(pTb[pi2],))
                        pts[(g, which)] = pi2
                return pts

            def pv(b, pts):
                po, pob = ps_next()
                pd, pdb = ps_next()
                for g in range(2):
                    terms_o = []
                    terms_d = []
                    for which in range(2):
                        pi2 = pts[(g, which)]
                        vblk = b + which
                        terms_o.append((vtm[:, vblk, g * 64:(g + 1) * 64], pT[pi2][:, :], (vcarb, vcurb, pTb[pi2])))
                        terms_d.append((ones[:, 0:64], pT[pi2][:, :], (constb, pTb[pi2])))
                    mm_group(po[g * 64:(g + 1) * 64, :], pob, terms_o)
                    mm_group(pd[g * 64:(g + 1) * 64, :], pdb, terms_d)
                i1 = nxt("tmp", 2)
                i2 = 2 + i1
                P.op("dve", lambda h, pd=pd: h.tensor_tensor(out=tmp[i2][:, :], in0=pd[:, :], in1=esink[l][:, :], op=ALU.add),
                     reads=(pdb, esinkb[l]), writes=(tmpb[i2],))
                P.op("act", lambda h: h.activation(out=tmp[i1][:, :], in_=tmp[i2][:, :], func=AF.Ln), reads=(tmpb[i2],), writes=(tmpb[i1],))
                P.op("act", lambda h: h.activation(out=tmp[i2][:, :], in_=tmp[i1][:, :], func=AF.Exp, scale=-1.0), reads=(tmpb[i1],), writes=(tmpb[i2],))
                ao_all = act32[:, 4 * 256:4 * 256 + 4 * 512].rearrange("p (j t) -> p j t", t=512)[:, :, b * 128:(b + 1) * 128]
                P.op("dve", lambda h, po=po: h.tensor_tensor(out=ao_all, in0=po[:, :].rearrange("p (j q) -> p j q", q=128),
                                                             in1=tmp[i2][:, :].rearrange("p (j q) -> p j q", q=128), op=ALU.mult),
                     reads=(pob, tmpb[i2]), writes=tuple(aob))
                P.op("act", lambda h: h.activation(out=sq[:, 4:8, b * 128:(b + 1) * 128], in_=ao_all, func=AF.Square),
                     reads=tuple(aob), writes=tuple(sqb[4:8]))

            pts_all = [scores(b) for b in range(min(2, nb))]
            for b in range(nb):
                if b + 2 < nb:
                    pts_all.append(scores(b + 2))
                pv(b, pts_all[b])
                if b == 0:
                    pool_mm()
                    bnorm_sq()
                if b == 1 or nb == 1:
                    bnorm()
            NOPEN = 6
            wslots = [load_piece(pcs["wout"][0]), load_piece(pcs["wout"][1])]

            def wterm(m, kc):
                slot, slb = wslots[m // 4]
                mm_ = m % 4
                return (slot[:, mm_ * 1024 + kc * 128: mm_ * 1024 + (kc + 1) * 128], mrgc(kc, T), (slb, mrgb[kc]))

            opened = []
            for m in range(NOPEN):
                pp, ppb = ps_next()
                mm_group(pp[:, :T], ppb, [wterm(m, kc) for kc in range(4, 8)], first=True, final=False)
                opened.append((pp, ppb))
            rA, rAb = rstd_from([], 512, T, skip_sq=True, chunks=(4, 5, 6, 7))
            for j in range(4):
                P.op("dve", lambda h, j=j: h.scalar_tensor_tensor(out=mrgc(j, T), in0=ao_ap(j)[:, :T], scalar=vcol(vb + 40 + j), in1=rA,
                                                                   op0=ALU.mult, op1=ALU.mult),
                     reads=(aob[j], rAb, constb), writes=(mrgb[j],))
            for m in range(8):
                if m < NOPEN:
                    pp, ppb = opened[m]
                    mm_group(pp[:, :T], ppb, [wterm(m, kc) for kc in range(0, 4)], first=False, final=True)
                else:
                    pp, ppb = ps_next()
                    mm_group(pp[:, :T], ppb, [wterm(m, kc) for kc in (4, 5, 6, 7, 0, 1, 2, 3)])
                P.op("dve", lambda h, pp=pp, m=m: h.tensor_tensor(out=xT[:, m, :T], in0=pp[:, :T], in1=xT[:, m, :T], op=ALU.add),
                     reads=(ppb, xb[m]), writes=(xb[m],))
            if T == TT:
                P.op("dve", lambda h: h.tensor_copy(out=kcarL[l][:, :], in_=kT[:, T:T + 128]), reads=(kcurb,), writes=(kcarLb[l],))
                P.op("dve", lambda h: h.tensor_copy(out=vcarL[l][:, :], in_=vtm[:, nb, :]), reads=(vcurb,), writes=(vcarLb[l],))
                for g in range(4):
                    P.op("dve", lambda h, g=g: h.tensor_copy(out=ucarL[l][:, g, :], in_=ubuf[:, g, T:T + 16]), reads=(ucurb[g],), writes=(ucarLb[l],))

        def memkv(l):
            pcs = PCS[l]
            vb = l * NVL
            P.dma("act", memx.rearrange("p (c m) -> p c m", c=NCH), memT_d, reads=(), writes=tuple(memxb))
            M = NMEM

            def mx(c):
                return memx[:, c * M:(c + 1) * M]
            r_ap, r_b = rstd_from([(mx(c), memxb[c]) for c in range(NCH)], D, M)
            for c in range(NCH):
                P.op("dve", lambda h, c=c: h.scalar_tensor_tensor(out=qx_ap(c)[:, :M], in0=mx(c), scalar=vcol(vb + 24 + c), in1=r_ap,
                                                                   op0=ALU.mult, op1=ALU.mult),
                     reads=(memxb[c], r_b, constb), writes=(qxb[c],))
            for pi_ in range(2):
                slot, slb = load_piece(pcs["wkvk"][pi_])
                for oo in range(4):
                    oc = pi_ * 4 + oo
                    pp, ppb = ps_next()
                    mm_group(pp[:, :M], ppb, [(slot[:, oo * 1024 + kc * 128: oo * 1024 + (kc + 1) * 128], qx_ap(kc)[:, :M], (slb, qxb[kc])) for kc in range(8)])
                    P.op("act", lambda h, pp=pp, oc=oc: h.activation(out=KTx[l][:, oc, :], in_=pp[:, :M], func=AF.Copy),
                         reads=(ppb,), writes=(KTxb[l],))
            for hf in range(2):
                slot, slb = load_piece(pcs["wkvv"][hf])
                for mc in range(2):
                    pp, ppb = ps_next()
                    mm_group(pp[:, :], ppb, [(qx_ap(kc)[:, mc * 128:(mc + 1) * 128], slot[:, kc * 512:(kc + 1) * 512], (slb, qxb[kc])) for kc in range(8)])
                    P.op("act", lambda h, pp=pp, mc=mc, hf=hf: h.activation(out=Vx[l][:, mc, hf * 512:(hf + 1) * 512], in_=pp[:, :], func=AF.Copy),
                         reads=(ppb,), writes=(Vxb[l],))

        def xattn(l, ti, T):
            pcs = PCS[l]
            vb = l * NVL
            if ti == 0:
                memkv(l)
            for c in range(NCH):
                P.op("act", lambda h, c=c: h.activation(out=hT[:, c, :T], in_=xT[:, c, :T], func=AF.Copy, scale=vcol(vb + 16 + c)),
                     reads=(xb[c], constb), writes=(hb[c],))
            xsrcs = [(xT[:, c, :T], xb[c]) for c in range(NCH)]
            rstd_sq(xsrcs, T)
            pend = []
            rq = {}
            for pi_ in range(2):
                slot, slb = load_piece(pcs["wq"][pi_])
                for oo in range(4):
                    oc = pi_ * 4 + oo
                    pp, ppb = ps_next()
                    mm_group(pp[:, :T], ppb, [(slot[:, oo * 1024 + kc * 128: oo * 1024 + (kc + 1) * 128], hT[:, kc, :T], (slb, hb[kc])) for kc in range(8)])
                    pend.append((pp, ppb, oc))
                if pi_ == 0:
                    rq["r"] = rstd_from(xsrcs, D, T, skip_sq=True)
                r_ap, r_b = rq["r"]
                for pp, ppb, oc in pend:
                    P.op("dve", lambda h, pp=pp, oc=oc: h.tensor_tensor(out=qx_ap(oc)[:, :T], in0=pp[:, :T], in1=r_ap, op=ALU.mult),
                         reads=(ppb, r_b), writes=(qxb[oc],))
                pend = []
            def xs(hd):
                pis = []
                for mc in range(2):
                    pS, pSb = ps_next()
                    mm_group(pS[:, :T], pSb, [(KTx[l][:, hd * 2 + dc, mc * 128:(mc + 1) * 128], qx_ap(hd * 2 + dc)[:, :T], (KTxb[l], qxb[hd * 2 + dc])) for dc in range(2)])
                    pi2 = nxt("pT", NPT)
                    P.op("act", lambda h, pS=pS, pi2=pi2: h.activation(out=pT[pi2][:, :T], in_=pS[:, :T], func=AF.Exp, scale=1.0 / 16.0),
                         reads=(pSb,), writes=(pTb[pi2],))
                    pis.append(pi2)
                return pis

            def xpv(hd, pis):
                pd, pdb = ps_next()
                mm_group(pd[:, :T], pdb, [(ones[:, :], pT[pis[mc]][:, :T], (constb, pTb[pis[mc]])) for mc in range(2)])
                i1 = nxt("tmp", 2)
                P.op("act", lambda h, pd=pd: h.activation(out=tmp[i1 + 2][:, :T], in_=pd[:, :T], func=AF.Ln), reads=(pdb,), writes=(tmpb[i1 + 2],))
                P.op("act", lambda h: h.activation(out=tmp[i1][:, :T], in_=tmp[i1 + 2][:, :T], func=AF.Exp, scale=-1.0), reads=(tmpb[i1 + 2],), writes=(tmpb[i1],))
                for dc in range(2):
                    po, pob = ps_next()
                    mm_group(po[:, :T], pob, [(Vx[l][:, mc, hd * 256 + dc * 128: hd * 256 + (dc + 1) * 128], pT[pis[mc]][:, :T], (Vxb[l], pTb[pis[mc]])) for mc in range(2)])
                    P.op("dve", lambda h, po=po, dc=dc: h.tensor_tensor(out=ox_ap(hd * 2 + dc)[:, :T], in0=po[:, :T], in1=tmp[i1][:, :T], op=ALU.mult),
                         reads=(pob, tmpb[i1]), writes=(oxb[hd * 2 + dc],))

            pis_cur = xs(0)
            for hd in range(4):
                pis_nxt = xs(hd + 1) if hd + 1 < 4 else None
                xpv(hd, pis_cur)
                pis_cur = pis_nxt
            for pi_ in range(2):
                slot, slb = load_piece(pcs["wo"][pi_])
                for mm_ in range(4):
                    m = pi_ * 4 + mm_
                    pp, ppb = ps_next()
                    mm_group(pp[:, :T], ppb, [(slot[:, mm_ * 1024 + kc * 128: mm_ * 1024 + (kc + 1) * 128], ox_ap(kc)[:, :T], (slb, oxb[kc])) for kc in range(8)])
                    P.op("dve", lambda h, pp=pp, m=m: h.tensor_tensor(out=xT[:, m, :T], in0=pp[:, :T], in1=xT[:, m, :T], op=ALU.add),
                         reads=(ppb, xb[m]), writes=(xb[m],))

        issue_conv(3)
        P.dma("sp", vecs[:, :], vecs_d, reads=(), writes=(constb,))
        P.dma("sp", invcf[:, :], invcf_d, reads=(), writes=(constb,))
        for (dst, src) in ((maskc, maskc_d), (maskp, maskp_d), (maskf, maskf_d), (ident, ident_d)):
            P.dma("pool", dst[:, :], src, reads=(), writes=(constb2,))
        P.op("dve", lambda h: h.memset(ones[:, :], 1.0), reads=(), writes=(constb,))
        for l in range(L):
            P.dma("sp", esink[l][:, :], sinkr_d[l], reads=(), writes=(esinkb[l],))
            P.op("act", lambda h, l=l: h.activation(out=esink[l][:, :], in_=esink[l][:, :], func=AF.Exp), reads=(esinkb[l],), writes=(esinkb[l],))
            P.op("dve", lambda h, l=l: h.memset(kcarL[l][:, :], 0.0), reads=(), writes=(kcarLb[l],))
            P.op("dve", lambda h, l=l: h.memset(vcarL[l][:, :], 0.0), reads=(), writes=(vcarLb[l],))
            P.op("dve", lambda h, l=l: h.memset(ucarL[l][:, :, :], 0.0), reads=(), writes=(ucarLb[l],))

        outb = Buf("yT")
        ntl = len(tiles)

        def load_x(ti):
            c0_, T_ = tiles[ti]
            P.dma("act", xTs[ti % 2][:, :, :T_], xT_d[:, :, c0_:c0_ + T_], reads=(), writes=tuple(xbs[ti % 2]))

        def final_sq(ti):
            c0_, T_ = tiles[ti]
            xT_, xb_ = xTs[ti % 2], xbs[ti % 2]
            if max(c0_, HALO) >= c0_ + T_:
                return
            rstd_sq([(xT_[:, c, :T_], xb_[c]) for c in range(NCH)], T_)

        def final_norm(ti, skip_sq=False):
            c0_, T_ = tiles[ti]
            xT_, xb_ = xTs[ti % 2], xbs[ti % 2]
            lo = max(c0_, HALO)
            if lo >= c0_ + T_:
                return
            r_ap, r_b = rstd_from([(xT_[:, c, :T_], xb_[c]) for c in range(NCH)], D, T_, skip_sq=skip_sq)
            for c in range(NCH):
                yi = nxt("yb", NYB)
                P.op("dve", lambda h, c=c, yi=yi: h.scalar_tensor_tensor(out=ybuf[yi][:, :T_], in0=xT_[:, c, :T_], scalar=vcol(V_FIN + c), in1=r_ap,
                                                                          op0=ALU.mult, op1=ALU.mult),
                     reads=(xb_[c], r_b, constb), writes=(ybufb[yi],))
                P.dma("pool", yT_d[:, c, lo - HALO:c0_ + T_ - HALO], ybuf[yi][:, lo - c0_:T_], reads=(ybufb[yi],), writes=(), box=ybox[yi])

        rot["yb"] = 0
        ybox = [P.new_box(f"ybox{i}") for i in range(NYB)]
        load_x(0)
        rope_tables(*tiles[0])
        for ti, (c0, T) in enumerate(tiles):
            xT, xb = xTs[ti % 2], xbs[ti % 2]

            def early_hook(ti=ti):
                if ti > 0:
                    final_norm(ti - 1, skip_sq=True)
                if ti + 1 < ntl:
                    load_x(ti + 1)

            def early_sq(ti=ti):
                if ti > 0:
                    final_sq(ti - 1)

            nxt_state = {}

            def nsrcs(ti=ti):
                Tn = tiles[ti + 1][1]
                xTn, xbn = xTs[(ti + 1) % 2], xbs[(ti + 1) % 2]
                return [(xTn[:, c, :Tn], xbn[c]) for c in range(NCH)], Tn, xTn, xbn

            def nx_sq(ti=ti):
                if ti + 1 < ntl:
                    srcs, Tn, _, _ = nsrcs()
                    rstd_sq(srcs, Tn)

            def nx_stat(ti=ti):
                if ti + 1 < ntl:
                    srcs, Tn, _, _ = nsrcs()
                    nxt_state["r"] = rstd_from(srcs, D, Tn, skip_sq=True)

            def mid_hook(ti=ti):
                if ti + 1 < ntl:
                    srcs, Tn, xTn, xbn = nsrcs()
                    r_ap, r_b = nxt_state["r"]
                    for c in range(NCH):
                        P.op("dve", lambda h, c=c: h.scalar_tensor_tensor(out=hT[:, c, :Tn], in0=xTn[:, c, :Tn], scalar=vcol(0 + c),
                                                                           in1=r_ap, op0=ALU.mult, op1=ALU.mult),
                             reads=(xbn[c], r_b, constb), writes=(hb[c],))

            for l in range(L):
                for st in range(4):
                    if ti == 0:
                        issue_conv(3 + (l * 6 + [0, 2, 3, 4][st]) + 2)
                    if st == 0:
                        ffn(l, 1, T, pre_normed=(ti > 0 and l == 0), hook_early=(early_hook if l == 0 else None),
                            hooks=({1: early_sq} if l == 0 else None))
                    elif st == 1:
                        mixer(l, ti, T)
                    elif st == 2:
                        xattn(l, ti, T)
                    else:
                        dr = rope_tables_gen(*tiles[ti + 1]) if (l == L - 1 and ti + 1 < ntl) else None
                        ffn(l, 2, T, hook_mid=(mid_hook if l == L - 1 else None), drip=dr,
                            hooks=({1: nx_sq, 5: nx_stat} if l == L - 1 else None))
            if ti == 0:
                for bx in wbbox:
                    if bx.cnt > 0:
                        P.E["sp"].h.wait_ge(bx.sem, bx.cnt)
                        P.E["sp"].seen[id(bx.sem)] = bx.cnt
            STATE["ti"] = ti + 1
        final_norm(ntl - 1)
        e = P.E["act"]
        for bx in ybox:
            e.h.wait_ge(bx.sem, bx.cnt)
        print(f"[build] n_own={n_own} tiles={len(tiles)} instructions~{P.ninst} sems={P.nsem}")
    return nc


def _prep_shared(inp):
    f32 = np.float32
    w_in = np.asarray(inp["w_in"], f32)
    cols = []

    def head_cols(base, h, rot):
        c = np.arange(base + h * 64, base + (h + 1) * 64)
        if rot:
            c = np.concatenate([c[32:], c[:32]])
        return c
    for c in range(4):
        cols.append(np.concatenate([head_cols(0, c, False), head_cols(0, c + 4, False)]))
        cols.append(np.concatenate([head_cols(0, c, True), head_cols(0, c + 4, True)]))
    cols.append(np.concatenate([head_cols(512, 0, False), head_cols(512, 1, False)]))
    cols.append(np.concatenate([head_cols(512, 0, True), head_cols(512, 1, True)]))
    for g in range(4):
        cols.append(np.arange(768 + g * 128, 768 + (g + 1) * 128))
    cols.append(np.arange(640, 768))
    cols = np.concatenate(cols)
    winx = np.ascontiguousarray(w_in[:, :, cols])
    arow = []
    for c in range(4):
        arow.append(np.arange(c * 64, (c + 1) * 64))
        arow.append(np.arange((c + 4) * 64, (c + 5) * 64))
    arow = np.concatenate(arow)
    rows = np.concatenate([arow, np.arange(512, 1024)])
    wout = np.ascontiguousarray(np.asarray(inp["w_out"], f32)[:, rows, :])

    vecs = np.zeros((128, NV), f32)

    def put(col0, v):
        v = np.asarray(v, f32)
        n = v.shape[0] // 128
        vecs[:, col0:col0 + n] = v.reshape(n, 128).T
    for l in range(L):
        b = l * NVL
        put(b + 0, inp["ffn1_norm"][l])
        put(b + 8, inp["mix_norm"][l])
        put(b + 16, inp["xattn_norm"][l])
        put(b + 24, inp["mem_norm"][l])
        put(b + 32, inp["ffn2_norm"][l])
        put(b + 40, np.asarray(inp["attn_out_norm"], f32)[l][arow])
        put(b + 44, inp["pool_out_norm"][l])
        put(b + 48, inp["pool_scale"][l])
    put(V_FIN, inp["final_norm"])
    p = np.arange(128)
    vecs[:, V_INVF] = (10000.0 ** (-(2.0 * (p % 32)) / 64.0)).astype(f32)
    sgn = np.where((p % 64) < 32, -1.0, 1.0).astype(f32)
    vecs[:, V_SGN] = sgn
    vecs[:, V_NPISGN] = (-PI * sgn).astype(f32)
    vecs[:, V_NPI] = -PI
    vecs[:, V_EPS] = EPS
    sinks = np.asarray(inp["attn_sinks"], f32)
    sinkr = np.zeros((L, 128, 512), f32)
    for l in range(L):
        for g in range(2):
            for j in range(4):
                sinkr[l, g * 64:(g + 1) * 64, j * 128:(j + 1) * 128] = sinks[l, g * 4 + j]
    kp = np.arange(128)[:, None]
    qq = np.tile(np.arange(128), 4)[None, :]
    maskc = np.where(kp <= qq, 0.0, NEG).astype(f32)
    maskp = np.where(kp > qq, 0.0, NEG).astype(f32)
    shared = dict(
        vecs=vecs, sinkr=sinkr, maskc=maskc, maskp=maskp, ident=np.eye(128, dtype=f32),
        f1g=np.asarray(inp["ffn1_w_gate"], f32), f1u=np.asarray(inp["ffn1_w_up"], f32), f1d=np.asarray(inp["ffn1_w_down"], f32),
        f2g=np.asarray(inp["ffn2_w_gate"], f32), f2u=np.asarray(inp["ffn2_w_up"], f32), f2d=np.asarray(inp["ffn2_w_down"], f32),
        winx=winx, poolw=np.asarray(inp["pool_w"], f32), wout=wout,
        wq=np.asarray(inp["xattn_wq"], f32), wkv=np.asarray(inp["xattn_wkv"], f32), wo=np.asarray(inp["xattn_wo"], f32),
    )
    return shared, maskp


def _prep_core(inp, shared, maskp, core, n_own):
    f32 = np.float32
    B, S = inp["x"].shape[0], inp["x"].shape[1]
    per_seq = S // n_own
    b = core // per_seq
    hf = core % per_seq
    s0 = hf * n_own
    NT = n_own + HALO
    x = np.asarray(inp["x"], f32)
    xs = np.zeros((NT, D), f32)
    pos = np.zeros((NT,), np.int32)
    if s0 == 0:
        xs[HALO:] = x[b, 0:n_own]
        pos[HALO:] = np.asarray(inp["positions"])[b, 0:n_own]
    else:
        xs[:] = x[b, s0 - HALO:s0 + n_own]
        pos[:] = np.asarray(inp["positions"])[b, s0 - HALO:s0 + n_own]
    xT = np.ascontiguousarray(xs.T.reshape(NCH, 128, NT).transpose(1, 0, 2))
    memT = np.ascontiguousarray(np.asarray(inp["mem"], f32)[b].T.reshape(NCH, 128, NMEM).transpose(1, 0, 2))
    posr = np.ascontiguousarray(np.broadcast_to(pos[None, :], (128, NT)))
    invcf = np.zeros((128, 80), f32)
    invcf[:, 64:80] = 0.0 if s0 == 0 else 1.0
    for g in range(4):
        w = 2 ** (g + 1)
        for i in range(16):
            invcf[:, g * 16 + i] = 1.0 / (min(i + 1, w) if s0 == 0 else w)
    maskf = np.full((128, 512), NEG, f32) if s0 == 0 else maskp
    d = dict(shared)
    d.update(xT=xT, memT=memT, posr=posr, invcf=invcf, maskf=maskf)
    return d


_CACHE = {}


def run(inp, n_cores=8, trace=False):
    B, S = inp["x"].shape[0], inp["x"].shape[1]
    n_own = (B * S) // n_cores
    assert S % n_own == 0 and n_own % 256 == 0
    if n_own not in _CACHE:
        _CACHE[n_own] = build_program(n_own)
    nc = _CACHE[n_own]
    shared, maskp = _prep_shared(inp)
    in_maps = [_prep_core(inp, shared, maskp, c, n_own) for c in range(n_cores)]
    res = run_bass_kernel_spmd(nc, in_maps, core_ids=list(range(n_cores)), trace=trace)
    out = np.zeros((B, S, D), np.float32)
    per_seq = S // n_own
    for c in range(n_cores):
        yT = np.asarray(res.results[c]["yT"])
        b = c // per_seq
        s0 = (c % per_seq) * n_own
        out[b, s0:s0 + n_own, :] = yT.transpose(2, 1, 0).reshape(n_own, D)
    return out, res


def kernel(**inputs):
    out, _ = run(inputs, 8)
    return out
```

```python
import math
import contextlib
import numpy as np
import concourse.bass as bass
import concourse.mybir as mybir
from concourse.bass_utils import run_bass_kernel_spmd

F32 = mybir.dt.float32
BF16 = mybir.dt.bfloat16
I32 = mybir.dt.int32
AF = mybir.ActivationFunctionType
ALU = mybir.AluOpType

D = 1024
DFF = 2816
NCH = 8
HCH = 22
L = 2
HALO = 256
TT = 512
NMEM = 256
EPS = 1e-6
NVL = 52
V_FIN = 104
V_INVF = 112
V_SGN = 113
V_NPISGN = 114
V_NPI = 115
V_EPS = 116
NV = 120
NEG = -30000.0
NSLOT = 5
NYB = 3
NPT = 12
PI = math.pi


class SemBox:
    def __init__(self, sem):
        self.sem = sem
        self.cnt = 0


class Buf:
    def __init__(self, name, aliases=()):
        self.name = name
        self.w = None
        self.r = {}
        self.aliases = list(aliases)
        self.sbox = None

    def add_reader(self, tok):
        k, v = tok
        if self.r.get(k, (None, 0))[1] < v:
            self.r[k] = tok


class Eng:
    def __init__(self, name, h, sem):
        self.name = name
        self.h = h
        self.sem = sem
        self.cnt = 0
        self.seen = {}


class Prog:
    def __init__(self, nc, es):
        self.nc = nc
        self.es = es
        self.nsem = 0
        self.E = {}
        for name, h in (("pe", nc.tensor), ("act", nc.scalar), ("dve", nc.vector), ("pool", nc.gpsimd), ("sp", nc.sync)):
            self.E[name] = Eng(name, h, self.new_sem("e_" + name))
        self.sems = {}
        for e in self.E.values():
            self.sems[id(e.sem)] = e.sem
        self.ninst = 0

    def new_sem(self, name):
        self.nsem += 1
        return self.es.enter_context(self.nc.semaphore(name))

    def new_box(self, name):
        s = self.new_sem(name)
        if not hasattr(self, "sems"):
            self.sems = {}
        self.sems[id(s)] = s
        return SemBox(s)

    def sb(self, name, shape, dt):
        return self.es.enter_context(self.nc.sbuf_tensor("sb_" + name, shape, dt))

    def _deps(self, reads, writes):
        deps = []
        for b in reads:
            if b.w is not None:
                deps.append(b.w)
        for b in writes:
            for bb in (b, *b.aliases):
                if bb.w is not None:
                    deps.append(bb.w)
                deps.extend(bb.r.values())
        return deps

    def _wait(self, e, deps):
        need = {}
        for k, v in deps:
            if k == id(e.sem) and e.name == "pe":
                continue
            if need.get(k, 0) < v:
                need[k] = v
        for k, v in need.items():
            if e.seen.get(k, 0) >= v:
                continue
            e.h.wait_ge(self.sems[k], v)
            e.seen[k] = v
            self.ninst += 1

    def op(self, en, fn, reads=(), writes=(), signal=True):
        e = self.E[en]
        self._wait(e, self._deps(reads, writes))
        inst = fn(e.h)
        self.ninst += 1
        if signal:
            inst.then_inc(e.sem, 1)
            e.cnt += 1
            tok = (id(e.sem), e.cnt)
        else:
            tok = (id(e.sem), e.cnt + 1)
        for b in writes:
            b.w = tok
            b.r = {}
        for b in reads:
            b.add_reader(tok)
        return tok

    def dma_multi(self, qn, pairs, reads=(), writes=(), box=None):
        e = self.E[qn]
        self._wait(e, self._deps(reads, writes))
        if box is None:
            b0 = writes[0]
            if b0.sbox is None:
                b0.sbox = self.new_box("d_" + b0.name)
            box = b0.sbox
        for out_ap, in_ap in pairs:
            e.h.dma_start(out=out_ap, in_=in_ap).then_inc(box.sem, 16)
            self.ninst += 1
            box.cnt += 16
        tok = (id(box.sem), box.cnt)
        for b in writes:
            b.w = tok
            b.r = {}
        for b in reads:
            b.add_reader(tok)
        return tok

    def dma(self, qn, out_ap, in_ap, reads=(), writes=(), box=None):
        e = self.E[qn]
        self._wait(e, self._deps(reads, writes))
        if box is None:
            b0 = writes[0]
            if b0.sbox is None:
                b0.sbox = self.new_box("d_" + b0.name)
            box = b0.sbox
        inst = e.h.dma_start(out=out_ap, in_=in_ap)
        inst.then_inc(box.sem, 16)
        self.ninst += 1
        box.cnt += 16
        tok = (id(box.sem), box.cnt)
        for b in writes:
            b.w = tok
            b.r = {}
        for b in reads:
            b.add_reader(tok)
        return tok


def build_program(n_own):
    NT = n_own + HALO
    tiles = []
    c = 0
    while c < NT:
        T = min(TT, NT - c)
        tiles.append((c, T))
        c += T

    nc = bass.Bass("TRN2", target_bir_lowering=False)

    def din(name, shape, dt=F32):
        return nc.dram_tensor(name, list(shape), dt, kind="ExternalInput").ap()

    xT_d = din("xT", [128, NCH, NT])
    memT_d = din("memT", [128, NCH, NMEM])
    posr_d = din("posr", [128, NT], I32)
    vecs_d = din("vecs", [128, NV])
    sinkr_d = din("sinkr", [L, 128, 512])
    maskc_d = din("maskc", [128, 512])
    maskp_d = din("maskp", [128, 512])
    maskf_d = din("maskf", [128, 512])
    ident_d = din("ident", [128, 128])
    invcf_d = din("invcf", [128, 80])
    fW = {}
    for f in (1, 2):
        fW[(f, "g")] = din(f"f{f}g", [L, D, DFF])
        fW[(f, "u")] = din(f"f{f}u", [L, D, DFF])
        fW[(f, "d")] = din(f"f{f}d", [L, DFF, D])
    winx_d = din("winx", [L, D, 1920])
    poolw_d = din("poolw", [L, 4, 128, 128])
    wout_d = din("wout", [L, D, D])
    wq_d = din("wq", [L, D, D])
    wkv_d = din("wkv", [L, D, 2 * D])
    wo_d = din("wo", [L, D, D])
    yT_d = nc.dram_tensor("yT", [128, NCH, n_own], F32, kind="ExternalOutput").ap()

    es = contextlib.ExitStack()
    with es:
        P = Prog(nc, es)

        class _PV:
            def __init__(self, fn):
                self.fn = fn

        class _PieceAP:
            def __getitem__(self, key):
                return _PV(lambda slot, key=key: slot[key])

        class Piece:
            def __init__(self, name, ncols):
                self.ap = _PieceAP()
                self.dram = None
                self.gbox = None
                self.name = name
                self.n = ncols
                self.buf = Buf(name)
                self.conv = []

        def kcp(src2d, kc):
            return src2d.rearrange("(kc p) i -> p kc i", p=128)

        def pview(piece, lo, kc, i):
            return _PV(lambda slot, lo=lo, kc=kc, i=i: slot[:, lo:lo + kc * i].rearrange("p (kc i) -> p kc i", kc=kc))

        groups = []
        PCS = {}
        for l in range(L):
            def ffn_pieces(f):
                gu = []
                for j in range(11):
                    pc = Piece(f"s_f{f}gu_{l}_{j}", 4096)
                    pc.conv.append((pview(pc, 0, 8, 256), kcp(fW[(f, "g")][l][:, j * 256:(j + 1) * 256], 8)))
                    pc.conv.append((pview(pc, 2048, 8, 256), kcp(fW[(f, "u")][l][:, j * 256:(j + 1) * 256], 8)))
                    gu.append(pc)
                dn = []
                for m in range(8):
                    pc = Piece(f"s_f{f}d_{l}_{m}", 22 * 128)
                    pc.conv.append((pview(pc, 0, 22, 128), kcp(fW[(f, "d")][l][:, m * 128:(m + 1) * 128], 22)))
                    dn.append(pc)
                return gu, dn

            def col_pieces(name, src, chunk_groups):
                out = []
                for gi, chunks in enumerate(chunk_groups):
                    pc = Piece(f"{name}_{l}_{gi}", len(chunks) * 1024)
                    for ci, ch in enumerate(chunks):
                        pc.conv.append((pview(pc, ci * 1024, 8, 128), kcp(src[l][:, ch * 128:(ch + 1) * 128], 8)))
                    out.append(pc)
                return out

            gu1, dn1 = ffn_pieces(1)
            groups.append((f"f1a_{l}", gu1[:6]))
            groups.append((f"f1b_{l}", gu1[6:] + dn1))
            win = col_pieces("s_win", winx_d, [[0, 1, 2, 3], [4, 5, 6, 7], [8, 9, 10, 11], [12, 13, 14]])
            pw = Piece(f"s_pw_{l}", 512)
            pw.conv.append((_PV(lambda slot: slot[:, 0:512].rearrange("c (g d) -> c g d", g=4)), poolw_d[l].rearrange("g c d -> c g d")))
            wout = col_pieces("s_wout", wout_d, [[0, 1, 2, 3], [4, 5, 6, 7]])
            groups.append((f"mix_{l}", win + [pw] + wout))
            wkvk = col_pieces("s_wkvk", wkv_d, [[0, 1, 2, 3], [4, 5, 6, 7]])
            wkvv = []
            for hf in range(2):
                pc = Piece(f"s_wkvv_{l}_{hf}", 4096)
                pc.conv.append((pview(pc, 0, 8, 512), kcp(wkv_d[l][:, D + hf * 512:D + (hf + 1) * 512], 8)))
                wkvv.append(pc)
            wq = col_pieces("s_wq", wq_d, [[0, 1, 2, 3], [4, 5, 6, 7]])
            wo = col_pieces("s_wo", wo_d, [[0, 1, 2, 3], [4, 5, 6, 7]])
            groups.append((f"xat_{l}", wkvk + wkvv + wq + wo))
            gu2, dn2 = ffn_pieces(2)
            groups.append((f"f2a_{l}", gu2[:6]))
            groups.append((f"f2b_{l}", gu2[6:] + dn2))
            PCS[l] = dict(gu1=gu1, dn1=dn1, win=win, pw=pw, wout=wout, wkvk=wkvk, wkvv=wkvv, wq=wq, wo=wo, gu2=gu2, dn2=dn2)

        conv_state = {"next": 0}
        ntiles_total = len(tiles)

        def issue_conv(upto):
            return

        STATE = {"ti": 0}
        for gname, pcs_ in groups:
            if gname.startswith("xat"):
                pcs_ = pcs_[4:]
            for pc in pcs_:
                pc.dram = nc.dram_tensor(pc.name, [128, pc.n], BF16, kind="Internal").ap()

        xTs = [P.sb(f"xT{i}", [128, NCH, TT], F32) for i in range(2)]
        xbs = [[Buf(f"x{i}_{c}") for c in range(NCH)] for i in range(2)]
        xT = xTs[0]
        xb = xbs[0]
        ybuf = [P.sb(f"ybuf{i}", [128, TT], F32) for i in range(NYB)]
        ybufb = [Buf(f"ybuf{i}") for i in range(NYB)]
        hT = P.sb("hT", [128, NCH, TT], BF16)
        hb = [Buf(f"h{c}") for c in range(NCH)]
        sq = P.sb("sq", [128, NCH, TT], BF16)
        sqb = [Buf(f"sq{c}") for c in range(NCH)]
        act2 = P.sb("act", [128, HCH * TT], BF16)
        actb = [Buf(f"act{k}") for k in range(HCH)]
        act32 = act2[:, :].bitcast(F32)

        def actc(k, T=TT):
            return act2[:, k * TT:k * TT + T]

        def act_f32(k0):
            return act32[:, k0 * 256:k0 * 256 + 512]

        sg = [P.sb(f"sg{i}", [128, TT], F32) for i in range(2)]
        sgb = [Buf(f"sg{i}") for i in range(2)]
        tmp = [P.sb(f"tmp{i}", [128, TT], F32) for i in range(6)]
        tmpb = [Buf(f"tmp{i}") for i in range(6)]
        rst = [P.sb(f"rst{i}", [128, TT], F32) for i in range(3)]
        rstb = [Buf(f"rst{i}") for i in range(3)]
        kT = P.sb("kT", [128, 640], BF16)
        kcarb, kcurb = Buf("kcar"), Buf("kcur")
        vtm = P.sb("vtm", [128, 5, 128], BF16)
        vcarb, vcurb = Buf("vcar"), Buf("vcur")
        ubuf = P.sb("ubuf", [128, 4, 528], F32)
        ucarb = [Buf(f"ucar{g}") for g in range(4)]
        ucurb = [Buf(f"ucur{g}") for g in range(4)]
        kcarL = [P.sb(f"kcarL{l}", [128, 128], BF16) for l in range(L)]
        vcarL = [P.sb(f"vcarL{l}", [128, 128], BF16) for l in range(L)]
        ucarL = [P.sb(f"ucarL{l}", [128, 4, 16], F32) for l in range(L)]
        kcarLb = [Buf(f"kcarL{l}") for l in range(L)]
        vcarLb = [Buf(f"vcarL{l}") for l in range(L)]
        ucarLb = [Buf(f"ucarL{l}") for l in range(L)]
        ws = [P.sb(f"ws{i}", [128, 528], F32) for i in range(2)]
        wsb = [Buf(f"ws{i}") for i in range(2)]
        pooled = P.sb("pooled", [128, 4, TT], BF16)
        pooledb = [Buf(f"pooled{g}") for g in range(4)]
        mrg2 = P.sb("mrg", [128, NCH * TT], BF16)
        mrgb = [Buf(f"mrg{c}") for c in range(NCH)]
        memx = mrg2[:, :].bitcast(F32)

        def mrgc(c, T=TT):
            return mrg2[:, c * TT:c * TT + T]
        pT = [P.sb(f"pT{i}", [128, TT], BF16) for i in range(NPT)]
        pTb = [Buf(f"pT{i}") for i in range(NPT)]
        cos2 = P.sb("cos2", [128, TT], F32)
        sinS = P.sb("sinS", [128, TT], F32)
        cosb, sinb = Buf("cos2"), Buf("sinS")
        maskc = P.sb("maskc", [128, 512], BF16)
        maskp = P.sb("maskp", [128, 512], BF16)
        maskf = P.sb("maskf", [128, 512], BF16)
        ident = P.sb("ident", [128, 128], BF16)
        ones = P.sb("ones", [128, 128], BF16)
        constb = Buf("consts")
        constb2 = Buf("consts2")
        esink = [P.sb(f"esink{l}", [128, 512], F32) for l in range(L)]
        esinkb = [Buf(f"esink{l}") for l in range(L)]
        KTx = [P.sb(f"KTx{l}", [128, NCH, NMEM], BF16) for l in range(L)]
        Vx = [P.sb(f"Vx{l}", [128, 2, D], BF16) for l in range(L)]
        KTxb = [Buf(f"KTx{l}") for l in range(L)]
        Vxb = [Buf(f"Vx{l}") for l in range(L)]
        vecs = P.sb("vecs", [128, NV], F32)
        dummy = P.sb("dummy", [128, 2], F32)
        dummyb = Buf("dummy")
        invcf = P.sb("invcf", [128, 80], F32)
        ring = [P.sb(f"ring{i}", [128, 4096], BF16) for i in range(NSLOT)]
        ringb = [Buf(f"ring{i}") for i in range(NSLOT)]
        ring_state = {"i": 0}
        wbbox = [P.new_box(f"wb{i}") for i in range(NSLOT)]
        ring_sw = [P.new_box(f"ringsw{i}") for i in range(NSLOT)]

        qrb = [Buf(f"qr{c}", aliases=[actb[c]]) for c in range(4)]
        aob = [Buf(f"ao{j}", aliases=[actb[4 + 2 * j], actb[5 + 2 * j]]) for j in range(4)]
        obb = [Buf(f"ob{g}", aliases=[actb[12 + 2 * g], actb[13 + 2 * g]]) for g in range(4)]
        qxb = [Buf(f"qx{c}", aliases=[actb[c]]) for c in range(NCH)]
        oxb = [Buf(f"ox{c}", aliases=[actb[8 + c]]) for c in range(NCH)]
        youtb = [Buf(f"yout{c}", aliases=[actb[2 * c], actb[2 * c + 1]]) for c in range(NCH)]
        memxb = [Buf(f"memx{c}", aliases=[mrgb[c]]) for c in range(NCH)]
        for k in range(HCH):
            al = []
            for lst in (qrb, aob, obb, qxb, oxb, youtb):
                for b in lst:
                    if actb[k] in b.aliases:
                        al.append(b)
            actb[k].aliases = al
        for c in range(NCH):
            mrgb[c].aliases = [memxb[c]]
        overlays = qrb + aob + obb + qxb + oxb + youtb
        for b in overlays:
            extra = []
            for o in overlays:
                if o is not b and any(a in b.aliases for a in o.aliases if a in actb):
                    extra.append(o)
            b.aliases = b.aliases + extra

        def qr_ap(c):
            return actc(c)

        def ao_ap(j):
            return act_f32(4 + 2 * j)

        def ob_ap(g):
            return act_f32(12 + 2 * g)

        def qx_ap(c):
            return actc(c)

        def ox_ap(c):
            return actc(8 + c)

        def yout_ap(c):
            return act_f32(2 * c)

        psum = [es.enter_context(nc.psum_tensor(f"ps{i}", [128, 512], F32)) for i in range(8)]
        psb = [Buf(f"ps{i}") for i in range(8)]
        ps_state = {"i": 0}

        def ps_next():
            i = ps_state["i"]
            ps_state["i"] = (i + 1) % 8
            return psum[i], psb[i]

        rot = {"sg": 0, "rst": 0, "pT": 0, "tmp": 0}

        def nxt(kind, n):
            i = rot[kind]
            rot[kind] = (i + 1) % n
            return i

        def vcol(c):
            return vecs[:, c:c + 1]

        def mm_group(out_ap, out_buf, terms, extra_reads=(), first=True, final=True):
            n = len(terms)
            for i, (lt, rh, rb) in enumerate(terms):
                last = (i == n - 1) and final
                st = (i == 0) and first
                if first and final:
                    P.op("pe", lambda h, lt=lt, rh=rh, st=st, last=last: h.matmul(out_ap, lt, rh, start=st, stop=last),
                         reads=list(rb) + list(extra_reads), writes=(out_buf,), signal=last)
                else:
                    P.op("pe", lambda h, lt=lt, rh=rh, st=st, last=last: h.matmul(out_ap, lt, rh, start=st, stop=last, skip_group_check=True),
                         reads=list(rb) + list(extra_reads), writes=(out_buf,), signal=last)

        def load_piece(pc):
            i = ring_state["i"]
            ring_state["i"] = (i + 1) % NSLOT
            if STATE["ti"] == 0 or pc.dram is None:
                P.dma_multi("pool", [(dv.fn(ring[i]), src) for dv, src in pc.conv], reads=(), writes=(ringb[i],), box=ring_sw[i])
                if pc.dram is not None and ntiles_total > 1:
                    P.dma("sp", pc.dram, ring[i][:, 0:pc.n], reads=(ringb[i],), writes=(), box=wbbox[i])
            else:
                P.dma("sp", ring[i][:, 0:pc.n], pc.dram, reads=(pc.buf,), writes=(ringb[i],))
            return ring[i], ringb[i]

        def rstd_sq(srcs, T):
            for c, (ap, b) in enumerate(srcs):
                P.op("act", lambda h, ap=ap, c=c: h.activation(out=sq[:, c, :T], in_=ap, func=AF.Square),
                     reads=(b,), writes=(sqb[c],))

        def rstd_from(srcs, Dn, T, skip_sq=False, chunks=None):
            n = len(srcs) if chunks is None else len(chunks)
            if not skip_sq:
                rstd_sq(srcs, T)
            pn, pnb = ps_next()
            cids = list(range(n)) if chunks is None else list(chunks)
            mm_group(pn[:, :T], pnb, [(ones[:, :], sq[:, c, :T], (sqb[c], constb)) for c in cids])
            i = nxt("rst", 3)
            j = nxt("rst", 3)
            P.op("act", lambda h: h.activation(out=rst[i][:, :T], in_=pn[:, :T], func=AF.Ln, bias=vcol(V_EPS), scale=1.0 / Dn),
                 reads=(pnb, constb), writes=(rstb[i],))
            P.op("act", lambda h: h.activation(out=rst[j][:, :T], in_=rst[i][:, :T], func=AF.Exp, scale=-0.5),
                 reads=(rstb[i],), writes=(rstb[j],))
            return rst[j][:, :T], rstb[j]

        def norm_x_to_h(gbase, T, xT_=None, xb_=None):
            xT_ = xT if xT_ is None else xT_
            xb_ = xb if xb_ is None else xb_
            r_ap, r_b = rstd_from([(xT_[:, c, :T], xb_[c]) for c in range(NCH)], D, T)
            for c in range(NCH):
                P.op("dve", lambda h, c=c: h.scalar_tensor_tensor(out=hT[:, c, :T], in0=xT_[:, c, :T], scalar=vcol(gbase + c),
                                                                   in1=r_ap, op0=ALU.mult, op1=ALU.mult),
                     reads=(xb_[c], r_b, constb), writes=(hb[c],))

        def ffn(l, f, T, pre_normed=False, hook_mid=None, hook_early=None, drip=None, hooks=None):
            pcs = PCS[l]
            gu, dn = (pcs["gu1"], pcs["dn1"]) if f == 1 else (pcs["gu2"], pcs["dn2"])
            if not pre_normed:
                norm_x_to_h(l * NVL + (0 if f == 1 else 32), T)
            for j in range(11):
                if j == 3 and hook_early is not None:
                    hook_early()
                if hooks is not None and j in hooks:
                    hooks[j]()
                slot, slb = load_piece(gu[j])
                for hh in range(2):
                    hc = 2 * j + hh
                    pg, pgb = ps_next()
                    pu, pub = ps_next()
                    mm_group(pg[:, :T], pgb, [(slot[:, kc * 256 + hh * 128: kc * 256 + hh * 128 + 128], hT[:, kc, :T], (slb, hb[kc])) for kc in range(8)])
                    mm_group(pu[:, :T], pub, [(slot[:, 2048 + kc * 256 + hh * 128: 2048 + kc * 256 + hh * 128 + 128], hT[:, kc, :T], (slb, hb[kc])) for kc in range(8)])
                    si = nxt("sg", 2)
                    P.op("act", lambda h, si=si, pg=pg: h.activation(out=sg[si][:, :T], in_=pg[:, :T], func=AF.Silu),
                         reads=(pgb,), writes=(sgb[si],))
                    P.op("dve", lambda h, si=si, pu=pu, hc=hc: h.tensor_tensor(out=actc(hc, T), in0=pu[:, :T], in1=sg[si][:, :T], op=ALU.mult),
                         reads=(pub, sgb[si]), writes=(actb[hc],))
                    if drip is not None:
                        next(drip, None)
            if drip is not None:
                for _ in drip:
                    pass
            if hook_mid is not None:
                hook_mid()
            P.op("act", lambda h: h.activation(out=dummy[:, 0:1], in_=vecs[:, V_EPS:V_EPS + 1], func=AF.Ln), reads=(constb,), writes=(dummyb,))
            for m in range(8):
                slot, slb = load_piece(dn[m])
                pd, pdb = ps_next()
                mm_group(pd[:, :T], pdb, [(slot[:, kc * 128:(kc + 1) * 128], actc(kc, T), (slb, actb[kc])) for kc in range(HCH)])
                P.op("dve", lambda h, pd=pd, m=m: h.scalar_tensor_tensor(out=xT[:, m, :T], in0=pd[:, :T], scalar=0.5, in1=xT[:, m, :T],
                                                                          op0=ALU.mult, op1=ALU.add),
                     reads=(pdb, xb[m]), writes=(xb[m],))

        def rope_tables_gen(c0, T):
            posi = tmp[5][:, :].bitcast(I32)
            P.dma("act", posi[:, :T], posr_d[:, c0:c0 + T], reads=(), writes=(tmpb[5],))
            yield
            posf, posfb = tmp[4], tmpb[4]
            P.op("dve", lambda h: h.tensor_copy(out=posf[:, :T], in_=posi[:, :T]), reads=(tmpb[5],), writes=(posfb,))
            yield
            for (tbl, tb, shift, is_sin) in ((sinS, sinb, PI, True), (cos2, cosb, 1.5 * PI, False)):
                a, ab = tmp[0], tmpb[0]
                n_, nb_ = tmp[1], tmpb[1]
                ni = tmp[2][:, :].bitcast(I32)
                nib = tmpb[2]
                P.op("dve", lambda h, shift=shift: h.tensor_scalar(out=a[:, :T], in0=posf[:, :T], scalar1=vcol(V_INVF), scalar2=shift,
                                                                    op0=ALU.mult, op1=ALU.add), reads=(posfb, constb), writes=(ab,))
                yield
                P.op("dve", lambda h: h.tensor_scalar(out=n_[:, :T], in0=a[:, :T], scalar1=1.0 / (2 * PI), scalar2=None, op0=ALU.mult),
                     reads=(ab,), writes=(nb_,))
                yield
                P.op("dve", lambda h: h.tensor_copy(out=ni[:, :T], in_=n_[:, :T]), reads=(nb_,), writes=(nib,))
                yield
                P.op("dve", lambda h: h.tensor_copy(out=n_[:, :T], in_=ni[:, :T]), reads=(nib,), writes=(nb_,))
                yield
                P.op("dve", lambda h: h.scalar_tensor_tensor(out=a[:, :T], in0=n_[:, :T], scalar=-2 * PI, in1=a[:, :T], op0=ALU.mult, op1=ALU.add),
                     reads=(nb_, ab), writes=(ab,))
                yield
                P.op("dve", lambda h: h.tensor_scalar(out=n_[:, :T], in0=a[:, :T], scalar1=0.0, scalar2=2 * PI, op0=ALU.is_lt, op1=ALU.mult),
                     reads=(ab,), writes=(nb_,))
                yield
                P.op("dve", lambda h: h.tensor_tensor(out=a[:, :T], in0=a[:, :T], in1=n_[:, :T], op=ALU.add), reads=(ab, nb_), writes=(ab,))
                yield
                P.op("dve", lambda h: h.tensor_scalar(out=n_[:, :T], in0=a[:, :T], scalar1=2 * PI, scalar2=-2 * PI, op0=ALU.is_ge, op1=ALU.mult),
                     reads=(ab,), writes=(nb_,))
                yield
                P.op("dve", lambda h: h.tensor_tensor(out=a[:, :T], in0=a[:, :T], in1=n_[:, :T], op=ALU.add), reads=(ab, nb_), writes=(ab,))
                yield
                if is_sin:
                    P.op("act", lambda h, tbl=tbl: h.activation(out=tbl[:, :T], in_=a[:, :T], func=AF.Sin, bias=vcol(V_NPISGN), scale=vcol(V_SGN)),
                         reads=(ab, constb), writes=(tb,))
                    yield
                else:
                    P.op("act", lambda h, tbl=tbl: h.activation(out=tbl[:, :T], in_=a[:, :T], func=AF.Sin, bias=vcol(V_NPI), scale=1.0),
                         reads=(ab, constb), writes=(tb,))
                    yield

        def rope_tables(c0, T):
            for _ in rope_tables_gen(c0, T):
                pass

        def mixer(l, ti, T):
            pcs = PCS[l]
            vb = l * NVL
            nb = T // 128
            P.op("dve", lambda h: h.tensor_copy(out=kT[:, 0:128], in_=kcarL[l][:, :]), reads=(kcarLb[l],), writes=(kcarb,))
            P.op("dve", lambda h: h.tensor_copy(out=vtm[:, 0, :], in_=vcarL[l][:, :]), reads=(vcarLb[l],), writes=(vcarb,))
            for g in range(4):
                P.op("dve", lambda h, g=g: h.tensor_copy(out=ubuf[:, g, 0:16], in_=ucarL[l][:, g, :]), reads=(ucarLb[l],), writes=(ucarb[g],))
            norm_x_to_h(vb + 8, T)
            slots = {}

            def proj(chunk_idx, slot, slb):
                pp, ppb = ps_next()
                mm_group(pp[:, :T], ppb, [(slot[:, chunk_idx * 1024 + kc * 128: chunk_idx * 1024 + (kc + 1) * 128], hT[:, kc, :T], (slb, hb[kc])) for kc in range(8)])
                return pp, ppb

            def rope(pq, pqb, pr, prb, out_ap, out_buf):
                i1 = 0 + nxt("tmp", 2)
                i2 = 2 + i1
                P.op("dve", lambda h: h.tensor_tensor(out=tmp[i1][:, :T], in0=pr[:, :T], in1=sinS[:, :T], op=ALU.mult),
                     reads=(prb, sinb), writes=(tmpb[i1],))
                P.op("dve", lambda h: h.tensor_tensor(out=tmp[i2][:, :T], in0=pq[:, :T], in1=cos2[:, :T], op=ALU.mult),
                     reads=(pqb, cosb), writes=(tmpb[i2],))
                P.op("dve", lambda h: h.tensor_tensor(out=out_ap, in0=tmp[i1][:, :T], in1=tmp[i2][:, :T], op=ALU.add),
                     reads=(tmpb[i1], tmpb[i2]), writes=(out_buf,))

            slot, slb = load_piece(pcs["win"][2])
            pq, pqb = proj(0, slot, slb)
            pr, prb = proj(1, slot, slb)
            rope(pq, pqb, pr, prb, kT[:, 128:128 + T], kcurb)
            ups = []
            for g in range(2):
                pp, ppb = proj(2 + g, slot, slb)
                P.op("dve", lambda h, pp=pp, g=g: h.tensor_copy(out=ubuf[:, g, 16:16 + T], in_=pp[:, :T]),
                     reads=(ppb,), writes=(ucurb[g],))
            slot, slb = load_piece(pcs["win"][3])
            for g in range(2, 4):
                pp, ppb = proj(g - 2, slot, slb)
                P.op("dve", lambda h, pp=pp, g=g: h.tensor_copy(out=ubuf[:, g, 16:16 + T], in_=pp[:, :T]),
                     reads=(ppb,), writes=(ucurb[g],))
            vslot, vslb = slot, slb
            if ti == 0:
                for g in range(4):
                    P.op("dve", lambda h, g=g: h.tensor_tensor(out=ubuf[:, g, 16 + 240:16 + 256], in0=ubuf[:, g, 16 + 240:16 + 256],
                                                               in1=invcf[:, 64:80], op=ALU.mult),
                         reads=(ucurb[g], constb), writes=(ucurb[g],))
            for pi_ in range(2):
                slot, slb = load_piece(pcs["win"][pi_])
                for cc in range(2):
                    c = pi_ * 2 + cc
                    pq, pqb = proj(cc * 2, slot, slb)
                    pr, prb = proj(cc * 2 + 1, slot, slb)
                    rope(pq, pqb, pr, prb, qr_ap(c)[:, :T], qrb[c])
            pv, pvb = ps_next()
            for b in range(nb):
                mm_group(pv[:, b * 128:(b + 1) * 128], pvb,
                         [(hT[:, kc, b * 128:(b + 1) * 128], vslot[:, 2048 + kc * 128: 2048 + (kc + 1) * 128], (vslb, hb[kc])) for kc in range(8)])
            P.op("dve", lambda h: h.tensor_copy(out=vtm[:, 1:1 + nb, :].rearrange("p b d -> p (b d)"), in_=pv[:, :nb * 128]),
                 reads=(pvb,), writes=(vcurb,))
            pwslot, pwslb = load_piece(pcs["pw"])
            for g in range(4):
                nlev = g + 1
                w = 2 ** nlev
                src, srcb = ubuf[:, g, :], (ucarb[g], ucurb[g])
                for lev in range(nlev):
                    step = 2 ** lev
                    lo = 2 ** (lev + 1) - 1
                    dst, dstb = ws[lev % 2], wsb[lev % 2]
                    P.op("dve", lambda h, src=src, dst=dst, lo=lo, step=step: h.tensor_tensor(
                        out=dst[:, lo:16 + T], in0=src[:, lo:16 + T], in1=src[:, lo - step:16 + T - step], op=ALU.add),
                        reads=srcb, writes=(dstb,))
                    src, srcb = dst, (dstb,)
                P.op("dve", lambda h, src=src, g=g, w=w: h.scalar_tensor_tensor(out=pooled[:, g, :T], in0=src[:, 16:16 + T], scalar=1.0 / w,
                                                                                in1=ubuf[:, g, 16:16 + T], op0=ALU.mult, op1=ALU.subtract),
                     reads=tuple(srcb) + (ucurb[g],), writes=(pooledb[g],))
                if ti == 0:
                    i1 = nxt("tmp", 2)
                    P.op("dve", lambda h, src=src, g=g: h.tensor_tensor(out=tmp[i1][:, 0:16], in0=src[:, 16 + 256:16 + 272], in1=invcf[:, g * 16:(g + 1) * 16], op=ALU.mult),
                         reads=tuple(srcb) + (constb,), writes=(tmpb[i1],))
                    P.op("dve", lambda h, g=g: h.tensor_tensor(out=pooled[:, g, 256:272], in0=tmp[i1][:, 0:16], in1=ubuf[:, g, 16 + 256:16 + 272], op=ALU.subtract),
                         reads=(tmpb[i1], ucurb[g]), writes=(pooledb[g],))
            def pool_mm():
                for g in range(4):
                    pm, pmb = ps_next()
                    mm_group(pm[:, :T], pmb, [(pwslot[:, g * 128:(g + 1) * 128], pooled[:, g, :T], (pwslb, pooledb[g]))])
                    P.op("dve", lambda h, pm=pm, g=g: h.tensor_scalar(out=ob_ap(g)[:, :T], in0=pm[:, :T], scalar1=vcol(vb + 48 + g), scalar2=None, op0=ALU.mult),
                         reads=(pmb, constb), writes=(obb[g],))

            def bnorm_sq():
                rstd_sq([(ob_ap(g)[:, :T], obb[g]) for g in range(4)], T)

            def bnorm():
                rB, rBb = rstd_from([(ob_ap(g)[:, :T], obb[g]) for g in range(4)], 512, T, skip_sq=True)
                for g in range(4):
                    P.op("dve", lambda h, g=g: h.scalar_tensor_tensor(out=mrgc(4 + g, T), in0=ob_ap(g)[:, :T], scalar=vcol(vb + 44 + g), in1=rB,
                                                                       op0=ALU.mult, op1=ALU.mult),
                         reads=(obb[g], rBb, constb), writes=(mrgb[4 + g],))
            def scores(b):
                gb = ti * (TT // 128) + b
                mprev = maskf if gb == 2 else maskp
                pts = {}
                for g in range(2):
                    rhs_q = act2[g * 64:(g + 1) * 64, 0:4 * TT].rearrange("p (c t) -> p c t", t=TT)[:, :, b * 128:(b + 1) * 128]
                    qreads = tuple(qrb)
                    for which in range(2):
                        pS, pSb = ps_next()
                        kcols = slice(b * 128 + which * 128, b * 128 + which * 128 + 128)
                        mk = mprev if which == 0 else maskc
                        kreads = (kcarb, kcurb)
                        mm_group(pS[:, :], pSb, [(ident[:, :], mk[:, :], (constb2,)),
                                                 (kT[g * 64:(g + 1) * 64, kcols], rhs_q, kreads + qreads)])
                        pi2 = nxt("pT", NPT)
                        P.op("act", lambda h, pS=pS, pi2=pi2: h.activation(out=pT[pi2][:, :], in_=pS[:, :], func=AF.Exp, scale=0.125),
                             reads=(pSb,), writes=(pTb[pi2],))
                        pts[(g, which)] = pi2
                return pts

            def pv(b, pts):
                po, pob = ps_next()
                pd, pdb = ps_next()
                for g in range(2):
                    terms_o = []
                    terms_d = []
                    for which in range(2):
                        pi2 = pts[(g, which)]
                        vblk = b + which
                        terms_o.append((vtm[:, vblk, g * 64:(g + 1) * 64], pT[pi2][:, :], (vcarb, vcurb, pTb[pi2])))
                        terms_d.append((ones[:, 0:64], pT[pi2][:, :], (constb, pTb[pi2])))
                    mm_group(po[g * 64:(g + 1) * 64, :], pob, terms_o)
                    mm_group(pd[g * 64:(g + 1) * 64, :], pdb, terms_d)
                i1 = nxt("tmp", 2)
                i2 = 2 + i1
                P.op("dve", lambda h, pd=pd: h.tensor_tensor(out=tmp[i2][:, :], in0=pd[:, :], in1=esink[l][:, :], op=ALU.add),
                     reads=(pdb, esinkb[l]), writes=(tmpb[i2],))
                P.op("act", lambda h: h.activation(out=tmp[i1][:, :], in_=tmp[i2][:, :], func=AF.Ln), reads=(tmpb[i2],), writes=(tmpb[i1],))
                P.op("act", lambda h: h.activation(out=tmp[i2][:, :], in_=tmp[i1][:, :], func=AF.Exp, scale=-1.0), reads=(tmpb[i1],), writes=(tmpb[i2],))
                ao_all = act32[:, 4 * 256:4 * 256 + 4 * 512].rearrange("p (j t) -> p j t", t=512)[:, :, b * 128:(b + 1) * 128]
                P.op("dve", lambda h, po=po: h.tensor_tensor(out=ao_all, in0=po[:, :].rearrange("p (j q) -> p j q", q=128),
                                                             in1=tmp[i2][:, :].rearrange("p (j q) -> p j q", q=128), op=ALU.mult),
                     reads=(pob, tmpb[i2]), writes=tuple(aob))
                P.op("act", lambda h: h.activation(out=sq[:, 4:8, b * 128:(b + 1) * 128], in_=ao_all, func=AF.Square),
                     reads=tuple(aob), writes=tuple(sqb[4:8]))

            pts_all = [scores(b) for b in range(min(2, nb))]
            for b in range(nb):
                if b + 2 < nb:
                    pts_all.append(scores(b + 2))
                pv(b, pts_all[b])
                if b == 0:
                    pool_mm()
                    bnorm_sq()
                if b == 1 or nb == 1:
                    bnorm()
            NOPEN = 7
            wslots = [load_piece(pcs["wout"][0]), load_piece(pcs["wout"][1])]

            def wterm(m, kc):
                slot, slb = wslots[m // 4]
                mm_ = m % 4
                return (slot[:, mm_ * 1024 + kc * 128: mm_ * 1024 + (kc + 1) * 128], mrgc(kc, T), (slb, mrgb[kc]))

            opened = []
            for m in range(NOPEN):
                pp, ppb = ps_next()
                mm_group(pp[:, :T], ppb, [wterm(m, kc) for kc in range(4, 8)], first=True, final=False)
                opened.append((pp, ppb))
            rA, rAb = rstd_from([], 512, T, skip_sq=True, chunks=(4, 5, 6, 7))
            for j in range(4):
                P.op("dve", lambda h, j=j: h.scalar_tensor_tensor(out=mrgc(j, T), in0=ao_ap(j)[:, :T], scalar=vcol(vb + 40 + j), in1=rA,
                                                                   op0=ALU.mult, op1=ALU.mult),
                     reads=(aob[j], rAb, constb), writes=(mrgb[j],))
            for m in range(8):
                if m < NOPEN:
                    pp, ppb = opened[m]
                    mm_group(pp[:, :T], ppb, [wterm(m, kc) for kc in range(0, 4)], first=False, final=True)
                else:
                    pp, ppb = ps_next()
                    mm_group(pp[:, :T], ppb, [wterm(m, kc) for kc in (4, 5, 6, 7, 0, 1, 2, 3)])
                P.op("dve", lambda h, pp=pp, m=m: h.tensor_tensor(out=xT[:, m, :T], in0=pp[:, :T], in1=xT[:, m, :T], op=ALU.add),
                     reads=(ppb, xb[m]), writes=(xb[m],))
            if T == TT:
                P.op("dve", lambda h: h.tensor_copy(out=kcarL[l][:, :], in_=kT[:, T:T + 128]), reads=(kcurb,), writes=(kcarLb[l],))
                P.op("dve", lambda h: h.tensor_copy(out=vcarL[l][:, :], in_=vtm[:, nb, :]), reads=(vcurb,), writes=(vcarLb[l],))
                for g in range(4):
                    P.op("dve", lambda h, g=g: h.tensor_copy(out=ucarL[l][:, g, :], in_=ubuf[:, g, T:T + 16]), reads=(ucurb[g],), writes=(ucarLb[l],))

        def memkv(l):
            pcs = PCS[l]
            vb = l * NVL
            P.dma("act", memx.rearrange("p (c m) -> p c m", c=NCH), memT_d, reads=(), writes=tuple(memxb))
            M = NMEM

            def mx(c):
                return memx[:, c * M:(c + 1) * M]
            r_ap, r_b = rstd_from([(mx(c), memxb[c]) for c in range(NCH)], D, M)
            for c in range(NCH):
                P.op("dve", lambda h, c=c: h.scalar_tensor_tensor(out=qx_ap(c)[:, :M], in0=mx(c), scalar=vcol(vb + 24 + c), in1=r_ap,
                                                                   op0=ALU.mult, op1=ALU.mult),
                     reads=(memxb[c], r_b, constb), writes=(qxb[c],))
            for pi_ in range(2):
                slot, slb = load_piece(pcs["wkvk"][pi_])
                for oo in range(4):
                    oc = pi_ * 4 + oo
                    pp, ppb = ps_next()
                    mm_group(pp[:, :M], ppb, [(slot[:, oo * 1024 + kc * 128: oo * 1024 + (kc + 1) * 128], qx_ap(kc)[:, :M], (slb, qxb[kc])) for kc in range(8)])
                    P.op("act", lambda h, pp=pp, oc=oc: h.activation(out=KTx[l][:, oc, :], in_=pp[:, :M], func=AF.Copy),
                         reads=(ppb,), writes=(KTxb[l],))
            for hf in range(2):
                slot, slb = load_piece(pcs["wkvv"][hf])
                for mc in range(2):
                    pp, ppb = ps_next()
                    mm_group(pp[:, :], ppb, [(qx_ap(kc)[:, mc * 128:(mc + 1) * 128], slot[:, kc * 512:(kc + 1) * 512], (slb, qxb[kc])) for kc in range(8)])
                    P.op("act", lambda h, pp=pp, mc=mc, hf=hf: h.activation(out=Vx[l][:, mc, hf * 512:(hf + 1) * 512], in_=pp[:, :], func=AF.Copy),
                         reads=(ppb,), writes=(Vxb[l],))

        def xattn(l, ti, T):
            pcs = PCS[l]
            vb = l * NVL
            if ti == 0:
                memkv(l)
            for c in range(NCH):
                P.op("act", lambda h, c=c: h.activation(out=hT[:, c, :T], in_=xT[:, c, :T], func=AF.Copy, scale=vcol(vb + 16 + c)),
                     reads=(xb[c], constb), writes=(hb[c],))
            xsrcs = [(xT[:, c, :T], xb[c]) for c in range(NCH)]
            rstd_sq(xsrcs, T)
            pend = []
            rq = {}
            for pi_ in range(2):
                slot, slb = load_piece(pcs["wq"][pi_])
                for oo in range(4):
                    oc = pi_ * 4 + oo
                    pp, ppb = ps_next()
                    mm_group(pp[:, :T], ppb, [(slot[:, oo * 1024 + kc * 128: oo * 1024 + (kc + 1) * 128], hT[:, kc, :T], (slb, hb[kc])) for kc in range(8)])
                    pend.append((pp, ppb, oc))
                if pi_ == 0:
                    rq["r"] = rstd_from(xsrcs, D, T, skip_sq=True)
                r_ap, r_b = rq["r"]
                for pp, ppb, oc in pend:
                    P.op("dve", lambda h, pp=pp, oc=oc: h.tensor_tensor(out=qx_ap(oc)[:, :T], in0=pp[:, :T], in1=r_ap, op=ALU.mult),
                         reads=(ppb, r_b), writes=(qxb[oc],))
                pend = []
            def xs(hd):
                pis = []
                for mc in range(2):
                    pS, pSb = ps_next()
                    mm_group(pS[:, :T], pSb, [(KTx[l][:, hd * 2 + dc, mc * 128:(mc + 1) * 128], qx_ap(hd * 2 + dc)[:, :T], (KTxb[l], qxb[hd * 2 + dc])) for dc in range(2)])
                    pi2 = nxt("pT", NPT)
                    P.op("act", lambda h, pS=pS, pi2=pi2: h.activation(out=pT[pi2][:, :T], in_=pS[:, :T], func=AF.Exp, scale=1.0 / 16.0),
                         reads=(pSb,), writes=(pTb[pi2],))
                    pis.append(pi2)
                return pis

            def xpv(hd, pis):
                pd, pdb = ps_next()
                mm_group(pd[:, :T], pdb, [(ones[:, :], pT[pis[mc]][:, :T], (constb, pTb[pis[mc]])) for mc in range(2)])
                i1 = nxt("tmp", 2)
                P.op("act", lambda h, pd=pd: h.activation(out=tmp[i1 + 2][:, :T], in_=pd[:, :T], func=AF.Ln), reads=(pdb,), writes=(tmpb[i1 + 2],))
                P.op("act", lambda h: h.activation(out=tmp[i1][:, :T], in_=tmp[i1 + 2][:, :T], func=AF.Exp, scale=-1.0), reads=(tmpb[i1 + 2],), writes=(tmpb[i1],))
                for dc in range(2):
                    po, pob = ps_next()
                    mm_group(po[:, :T], pob, [(Vx[l][:, mc, hd * 256 + dc * 128: hd * 256 + (dc + 1) * 128], pT[pis[mc]][:, :T], (Vxb[l], pTb[pis[mc]])) for mc in range(2)])
                    P.op("dve", lambda h, po=po, dc=dc: h.tensor_tensor(out=ox_ap(hd * 2 + dc)[:, :T], in0=po[:, :T], in1=tmp[i1][:, :T], op=ALU.mult),
                         reads=(pob, tmpb[i1]), writes=(oxb[hd * 2 + dc],))

            pis_cur = xs(0)
            for hd in range(4):
                pis_nxt = xs(hd + 1) if hd + 1 < 4 else None
                xpv(hd, pis_cur)
                pis_cur = pis_nxt
            for pi_ in range(2):
                slot, slb = load_piece(pcs["wo"][pi_])
                for mm_ in range(4):
                    m = pi_ * 4 + mm_
                    pp, ppb = ps_next()
                    mm_group(pp[:, :T], ppb, [(slot[:, mm_ * 1024 + kc * 128: mm_ * 1024 + (kc + 1) * 128], ox_ap(kc)[:, :T], (slb, oxb[kc])) for kc in range(8)])
                    P.op("dve", lambda h, pp=pp, m=m: h.tensor_tensor(out=xT[:, m, :T], in0=pp[:, :T], in1=xT[:, m, :T], op=ALU.add),
                         reads=(ppb, xb[m]), writes=(xb[m],))

        issue_conv(3)
        P.dma("sp", vecs[:, :], vecs_d, reads=(), writes=(constb,))
        P.dma("sp", invcf[:, :], invcf_d, reads=(), writes=(constb,))
        for (dst, src) in ((maskc, maskc_d), (maskp, maskp_d), (maskf, maskf_d), (ident, ident_d)):
            P.dma("pool", dst[:, :], src, reads=(), writes=(constb2,))
        P.op("dve", lambda h: h.memset(ones[:, :], 1.0), reads=(), writes=(constb,))
        for l in range(L):
            P.dma("sp", esink[l][:, :], sinkr_d[l], reads=(), writes=(esinkb[l],))
            P.op("act", lambda h, l=l: h.activation(out=esink[l][:, :], in_=esink[l][:, :], func=AF.Exp), reads=(esinkb[l],), writes=(esinkb[l],))
            P.op("dve", lambda h, l=l: h.memset(kcarL[l][:, :], 0.0), reads=(), writes=(kcarLb[l],))
            P.op("dve", lambda h, l=l: h.memset(vcarL[l][:, :], 0.0), reads=(), writes=(vcarLb[l],))
            P.op("dve", lambda h, l=l: h.memset(ucarL[l][:, :, :], 0.0), reads=(), writes=(ucarLb[l],))

        outb = Buf("yT")
        ntl = len(tiles)

        def load_x(ti):
            c0_, T_ = tiles[ti]
            P.dma("act", xTs[ti % 2][:, :, :T_], xT_d[:, :, c0_:c0_ + T_], reads=(), writes=tuple(xbs[ti % 2]))

        def final_sq(ti):
            c0_, T_ = tiles[ti]
            xT_, xb_ = xTs[ti % 2], xbs[ti % 2]
            if max(c0_, HALO) >= c0_ + T_:
                return
            rstd_sq([(xT_[:, c, :T_], xb_[c]) for c in range(NCH)], T_)

        def final_norm(ti, skip_sq=False):
            c0_, T_ = tiles[ti]
            xT_, xb_ = xTs[ti % 2], xbs[ti % 2]
            lo = max(c0_, HALO)
            if lo >= c0_ + T_:
                return
            r_ap, r_b = rstd_from([(xT_[:, c, :T_], xb_[c]) for c in range(NCH)], D, T_, skip_sq=skip_sq)
            for c in range(NCH):
                yi = nxt("yb", NYB)
                P.op("dve", lambda h, c=c, yi=yi: h.scalar_tensor_tensor(out=ybuf[yi][:, :T_], in0=xT_[:, c, :T_], scalar=vcol(V_FIN + c), in1=r_ap,
                                                                          op0=ALU.mult, op1=ALU.mult),
                     reads=(xb_[c], r_b, constb), writes=(ybufb[yi],))
                P.dma("pool", yT_d[:, c, lo - HALO:c0_ + T_ - HALO], ybuf[yi][:, lo - c0_:T_], reads=(ybufb[yi],), writes=(), box=ybox[yi])

        rot["yb"] = 0
        ybox = [P.new_box(f"ybox{i}") for i in range(NYB)]
        load_x(0)
        rope_tables(*tiles[0])
        for ti, (c0, T) in enumerate(tiles):
            xT, xb = xTs[ti % 2], xbs[ti % 2]

            def early_hook(ti=ti):
                if ti > 0:
                    final_norm(ti - 1, skip_sq=True)
                if ti + 1 < ntl:
                    load_x(ti + 1)

            def early_sq(ti=ti):
                if ti > 0:
                    final_sq(ti - 1)

            nxt_state = {}

            def nsrcs(ti=ti):
                Tn = tiles[ti + 1][1]
                xTn, xbn = xTs[(ti + 1) % 2], xbs[(ti + 1) % 2]
                return [(xTn[:, c, :Tn], xbn[c]) for c in range(NCH)], Tn, xTn, xbn

            def nx_sq(ti=ti):
                if ti + 1 < ntl:
                    srcs, Tn, _, _ = nsrcs()
                    rstd_sq(srcs, Tn)

            def nx_stat(ti=ti):
                if ti + 1 < ntl:
                    srcs, Tn, _, _ = nsrcs()
                    nxt_state["r"] = rstd_from(srcs, D, Tn, skip_sq=True)

            def mid_hook(ti=ti):
                if ti + 1 < ntl:
                    srcs, Tn, xTn, xbn = nsrcs()
                    r_ap, r_b = nxt_state["r"]
                    for c in range(NCH):
                        P.op("dve", lambda h, c=c: h.scalar_tensor_tensor(out=hT[:, c, :Tn], in0=xTn[:, c, :Tn], scalar=vcol(0 + c),
                                                                           in1=r_ap, op0=ALU.mult, op1=ALU.mult),
                             reads=(xbn[c], r_b, constb), writes=(hb[c],))

            for l in range(L):
                for st in range(4):
                    if ti == 0:
                        issue_conv(3 + (l * 6 + [0, 2, 3, 4][st]) + 2)
                    if st == 0:
                        ffn(l, 1, T, pre_normed=(ti > 0 and l == 0), hook_early=(early_hook if l == 0 else None),
                            hooks=({1: early_sq} if l == 0 else None))
                    elif st == 1:
                        mixer(l, ti, T)
                    elif st == 2:
                        xattn(l, ti, T)
                    else:
                        dr = rope_tables_gen(*tiles[ti + 1]) if (l == L - 1 and ti + 1 < ntl) else None
                        ffn(l, 2, T, hook_mid=(mid_hook if l == L - 1 else None), drip=dr,
                            hooks=({1: nx_sq, 5: nx_stat} if l == L - 1 else None))
            if ti == 0:
                for bx in wbbox:
                    if bx.cnt > 0:
                        P.E["sp"].h.wait_ge(bx.sem, bx.cnt)
                        P.E["sp"].seen[id(bx.sem)] = bx.cnt
            STATE["ti"] = ti + 1
        final_norm(ntl - 1)
        e = P.E["act"]
        for bx in ybox:
            e.h.wait_ge(bx.sem, bx.cnt)
        print(f"[build] n_own={n_own} tiles={len(tiles)} instructions~{P.ninst} sems={P.nsem}")
    return nc


def _prep_shared(inp):
    f32 = np.float32
    w_in = np.asarray(inp["w_in"], f32)
    cols = []

    def head_cols(base, h, rot):
        c = np.arange(base + h * 64, base + (h + 1) * 64)
        if rot:
            c = np.concatenate([c[32:], c[:32]])
        return c
    for c in range(4):
        cols.append(np.concatenate([head_cols(0, c, False), head_cols(0, c + 4, False)]))
        cols.append(np.concatenate([head_cols(0, c, True), head_cols(0, c + 4, True)]))
    cols.append(np.concatenate([head_cols(512, 0, False), head_cols(512, 1, False)]))
    cols.append(np.concatenate([head_cols(512, 0, True), head_cols(512, 1, True)]))
    for g in range(4):
        cols.append(np.arange(768 + g * 128, 768 + (g + 1) * 128))
    cols.append(np.arange(640, 768))
    cols = np.concatenate(cols)
    winx = np.ascontiguousarray(w_in[:, :, cols])
    arow = []
    for c in range(4):
        arow.append(np.arange(c * 64, (c + 1) * 64))
        arow.append(np.arange((c + 4) * 64, (c + 5) * 64))
    arow = np.concatenate(arow)
    rows = np.concatenate([arow, np.arange(512, 1024)])
    wout = np.ascontiguousarray(np.asarray(inp["w_out"], f32)[:, rows, :])

    vecs = np.zeros((128, NV), f32)

    def put(col0, v):
        v = np.asarray(v, f32)
        n = v.shape[0] // 128
        vecs[:, col0:col0 + n] = v.reshape(n, 128).T
    for l in range(L):
        b = l * NVL
        put(b + 0, inp["ffn1_norm"][l])
        put(b + 8, inp["mix_norm"][l])
        put(b + 16, inp["xattn_norm"][l])
        put(b + 24, inp["mem_norm"][l])
        put(b + 32, inp["ffn2_norm"][l])
        put(b + 40, np.asarray(inp["attn_out_norm"], f32)[l][arow])
        put(b + 44, inp["pool_out_norm"][l])
        put(b + 48, inp["pool_scale"][l])
    put(V_FIN, inp["final_norm"])
    p = np.arange(128)
    vecs[:, V_INVF] = (10000.0 ** (-(2.0 * (p % 32)) / 64.0)).astype(f32)
    sgn = np.where((p % 64) < 32, -1.0, 1.0).astype(f32)
    vecs[:, V_SGN] = sgn
    vecs[:, V_NPISGN] = (-PI * sgn).astype(f32)
    vecs[:, V_NPI] = -PI
    vecs[:, V_EPS] = EPS
    sinks = np.asarray(inp["attn_sinks"], f32)
    sinkr = np.zeros((L, 128, 512), f32)
    for l in range(L):
        for g in range(2):
            for j in range(4):
                sinkr[l, g * 64:(g + 1) * 64, j * 128:(j + 1) * 128] = sinks[l, g * 4 + j]
    kp = np.arange(128)[:, None]
    qq = np.tile(np.arange(128), 4)[None, :]
    maskc = np.where(kp <= qq, 0.0, NEG).astype(f32)
    maskp = np.where(kp > qq, 0.0, NEG).astype(f32)
    shared = dict(
        vecs=vecs, sinkr=sinkr, maskc=maskc, maskp=maskp, ident=np.eye(128, dtype=f32),
        f1g=np.asarray(inp["ffn1_w_gate"], f32), f1u=np.asarray(inp["ffn1_w_up"], f32), f1d=np.asarray(inp["ffn1_w_down"], f32),
        f2g=np.asarray(inp["ffn2_w_gate"], f32), f2u=np.asarray(inp["ffn2_w_up"], f32), f2d=np.asarray(inp["ffn2_w_down"], f32),
        winx=winx, poolw=np.asarray(inp["pool_w"], f32), wout=wout,
        wq=np.asarray(inp["xattn_wq"], f32), wkv=np.asarray(inp["xattn_wkv"], f32), wo=np.asarray(inp["xattn_wo"], f32),
    )
    return shared, maskp


def _prep_core(inp, shared, maskp, core, n_own):
    f32 = np.float32
    B, S = inp["x"].shape[0], inp["x"].shape[1]
    per_seq = S // n_own
    b = core // per_seq
    hf = core % per_seq
    s0 = hf * n_own
    NT = n_own + HALO
    x = np.asarray(inp["x"], f32)
    xs = np.zeros((NT, D), f32)
    pos = np.zeros((NT,), np.int32)
    if s0 == 0:
        xs[HALO:] = x[b, 0:n_own]
        pos[HALO:] = np.asarray(inp["positions"])[b, 0:n_own]
    else:
        xs[:] = x[b, s0 - HALO:s0 + n_own]
        pos[:] = np.asarray(inp["positions"])[b, s0 - HALO:s0 + n_own]
    xT = np.ascontiguousarray(xs.T.reshape(NCH, 128, NT).transpose(1, 0, 2))
    memT = np.ascontiguousarray(np.asarray(inp["mem"], f32)[b].T.reshape(NCH, 128, NMEM).transpose(1, 0, 2))
    posr = np.ascontiguousarray(np.broadcast_to(pos[None, :], (128, NT)))
    invcf = np.zeros((128, 80), f32)
    invcf[:, 64:80] = 0.0 if s0 == 0 else 1.0
    for g in range(4):
        w = 2 ** (g + 1)
        for i in range(16):
            invcf[:, g * 16 + i] = 1.0 / (min(i + 1, w) if s0 == 0 else w)
    maskf = np.full((128, 512), NEG, f32) if s0 == 0 else maskp
    d = dict(shared)
    d.update(xT=xT, memT=memT, posr=posr, invcf=invcf, maskf=maskf)
    return d


_CACHE = {}


def run(inp, n_cores=8, trace=False):
    B, S = inp["x"].shape[0], inp["x"].shape[1]
    n_own = (B * S) // n_cores
    assert S % n_own == 0 and n_own % 256 == 0
    if n_own not in _CACHE:
        _CACHE[n_own] = build_program(n_own)
    nc = _CACHE[n_own]
    shared, maskp = _prep_shared(inp)
    in_maps = [_prep_core(inp, shared, maskp, c, n_own) for c in range(n_cores)]
    res = run_bass_kernel_spmd(nc, in_maps, core_ids=list(range(n_cores)), trace=trace)
    out = np.zeros((B, S, D), np.float32)
    per_seq = S // n_own
    for c in range(n_cores):
        yT = np.asarray(res.results[c]["yT"])
        b = c // per_seq
        s0 = (c % per_seq) * n_own
        out[b, s0:s0 + n_own, :] = yT.transpose(2, 1, 0).reshape(n_own, D)
    return out, res


def kernel(**inputs):
    out, _ = run(inputs, 8)
    return out
```
